# Optimizing a Trainium2 kernel written in Bass

```python
import math
import jax
import jax.numpy as jnp
from jax import lax
import numpy as np

D_MODEL = 1024
BATCH = 8
SEQ = 4096
DEPTH = 1
DEC_BATCH = 8
DEC_SEQ = 64
PAST_LEN = 2048

CHUNK = 64
Q_BLOCK = 128
SSD_HEADS = 8
SSD_HEAD_DIM = 64
SSD_WIDTH = SSD_HEADS * SSD_HEAD_DIM
SSD_GROUPS = 2
SSD_STATE = 128
SSD_CONV = 4
SSD_BLOCK = CHUNK
CONV_CH = SSD_WIDTH + 2 * SSD_GROUPS * SSD_STATE
ATT_HEADS = 4
ATT_HEAD_DIM = 64
ATT_WIDTH = ATT_HEADS * 2 * ATT_HEAD_DIM
MIX_WIDTH = SSD_WIDTH + ATT_WIDTH
IN_COLS = SSD_WIDTH + CONV_CH + SSD_HEADS + 3 * ATT_WIDTH
IN_SPLITS = (SSD_WIDTH,
             SSD_WIDTH + CONV_CH,
             SSD_WIDTH + CONV_CH + SSD_HEADS,
             SSD_WIDTH + CONV_CH + SSD_HEADS + ATT_WIDTH,
             SSD_WIDTH + CONV_CH + SSD_HEADS + 2 * ATT_WIDTH)
D_FF = 2816
N_MOD = 9
EPS = 1e-6

kernel_name = 'hymba_ssd_diffattn_macaron_stream'


def rmsnorm(x, w):
    xf = x.astype(jnp.float32)
    y = xf * lax.rsqrt(jnp.mean(xf * xf, axis=-1, keepdims=True) + EPS)
    return (y * w.astype(jnp.float32)).astype(x.dtype)


def rmsnorm_plain(x):
    xf = x.astype(jnp.float32)
    return xf * lax.rsqrt(jnp.mean(xf * xf, axis=-1, keepdims=True) + EPS)


def modulate(x, shift, scale):
    return x * (1.0 + scale[:, None, :]) + shift[:, None, :]


def swiglu(x, w_gu, w_down):
    g, u = jnp.split(x @ w_gu, 2, axis=-1)
    return (jax.nn.silu(g) * u) @ w_down


def causal_dwconv(xpad, w, b):
    y = lax.conv_general_dilated(xpad, w[:, None, :].astype(xpad.dtype), (1,), 'VALID',
                                 dimension_numbers=('NWC', 'WIO', 'NWC'),
                                 feature_group_count=xpad.shape[-1])
    return y + b


def ssd_scan(x, dt, A, B, C, h0):
    b, L, H, P = x.shape
    G, N = B.shape[-2:]
    R = H // G
    blk = min(SSD_BLOCK, L)
    nc = L // blk
    X = (x * dt[..., None]).reshape(b, nc, blk, G, R, P)
    dA = (dt * A).reshape(b, nc, blk, G, R)
    Bc = B.reshape(b, nc, blk, G, N)
    Cc = C.reshape(b, nc, blk, G, N)
    Acs = jnp.cumsum(dA, axis=2)
    causal = jnp.tril(jnp.ones((blk, blk), bool))[None, None, :, :, None, None]
    seg = Acs[:, :, :, None] - Acs[:, :, None, :]
    Lmat = jnp.exp(jnp.where(causal, seg, -jnp.inf))
    CB = jnp.einsum('bclgn,bcsgn->bclsg', Cc, Bc)
    y_diag = jnp.einsum('bclsg,bclsgr,bcsgrp->bclgrp', CB, Lmat, X)
    decay = jnp.exp(Acs[:, :, -1:] - Acs)
    st = jnp.einsum('bclgn,bclgr,bclgrp->bcgrpn', Bc, decay, X)
    chunk_decay = jnp.exp(Acs[:, :, -1])

    def step(h, inp):
        s_c, d_c = inp
        return h * d_c[..., None, None] + s_c, h

    h_last, h_in = lax.scan(step, h0.reshape(b, G, R, P, N),
                            (jnp.moveaxis(st, 1, 0), jnp.moveaxis(chunk_decay, 1, 0)))
    h_in = jnp.moveaxis(h_in, 0, 1)
    y_off = jnp.einsum('bclgn,bcgrpn,bclgr->bclgrp', Cc, h_in, jnp.exp(Acs))
    y = (y_diag + y_off).reshape(b, L, H, P)
    return y, h_last.reshape(b, H, P, N)


def ssd_mixer(z, xpad, dt_raw, conv_w, conv_b, dt_bias, a_log, d_skip, ssd_norm, h0):
    f32 = jnp.float32
    xBC = jax.nn.silu(causal_dwconv(xpad, conv_w, conv_b))
    b, L, _ = xBC.shape
    GN = SSD_GROUPS * SSD_STATE
    xs = xBC[..., :SSD_WIDTH].reshape(b, L, SSD_HEADS, SSD_HEAD_DIM).astype(f32)
    Bm = xBC[..., SSD_WIDTH:SSD_WIDTH + GN].reshape(b, L, SSD_GROUPS, SSD_STATE).astype(f32)
    Cm = xBC[..., SSD_WIDTH + GN:].reshape(b, L, SSD_GROUPS, SSD_STATE).astype(f32)
    dt = jax.nn.softplus(dt_raw.astype(f32) + dt_bias.astype(f32))
    A = -jnp.exp(a_log.astype(f32))
    y, h_last = ssd_scan(xs, dt, A, Bm, Cm, h0.astype(f32))
    y = y + d_skip.astype(f32)[:, None] * xs
    y = y.reshape(b, L, SSD_WIDTH) * jax.nn.silu(z.astype(f32))
    y = rmsnorm(y, ssd_norm)
    return y.astype(z.dtype), h_last


def diff_attn(q, k, v, lam, mask):
    s = jnp.einsum('bqhjd,bkhjd->bhjqk', q.astype(jnp.float32), k.astype(jnp.float32))
    s = s * (1.0 / math.sqrt(ATT_HEAD_DIM))
    if mask is not None:
        s = jnp.where(mask, s, -jnp.inf)
    p = jax.nn.softmax(s, axis=-1)
    a = p[:, :, 0] - lam * p[:, :, 1]
    return jnp.einsum('bhqk,bkhe->bqhe', a, v.astype(jnp.float32))


def diff_attn_prompt(q, k, v, lam):
    b, L = q.shape[:2]
    nblk = L // Q_BLOCK
    kchunk = jnp.arange(L) // CHUNK

    def block(i):
        start = i * Q_BLOCK
        qb = lax.dynamic_slice_in_dim(q, start, Q_BLOCK, axis=1)
        qchunk = (start + jnp.arange(Q_BLOCK)) // CHUNK
        mask = kchunk[None, :] <= qchunk[:, None]
        return diff_attn(qb, k, v, lam, mask)

    out = lax.map(block, jnp.arange(nblk))
    return jnp.moveaxis(out, 0, 1).reshape(b, L, ATT_HEADS, 2 * ATT_HEAD_DIM)


def layer(x, c, p, layer_idx, past):
    b, L, _ = x.shape
    mod = (jax.nn.silu(c) @ p['w_ada'] + p['b_ada']).reshape(b, N_MOD, D_MODEL)
    sh1, sc1, g1 = mod[:, 0], mod[:, 1], mod[:, 2]
    sh2, sc2, g2 = mod[:, 3], mod[:, 4], mod[:, 5]
    sh3, sc3, g3 = mod[:, 6], mod[:, 7], mod[:, 8]
    h = modulate(rmsnorm(x, p['norm1']), sh1, sc1)
    x = x + 0.5 * g1[:, None, :] * swiglu(h, p['ffn1_wgu'], p['ffn1_wd'])
    u = modulate(rmsnorm(x, p['norm2']), sh2, sc2)
    z, xBC, dt_raw, q, k, v = jnp.split(u @ p['w_in'], IN_SPLITS, axis=-1)
    if past is None:
        conv_prefix = jnp.zeros((b, SSD_CONV - 1, CONV_CH), xBC.dtype)
        h0 = jnp.zeros((b, SSD_HEADS, SSD_HEAD_DIM, SSD_STATE), jnp.float32)
    else:
        k_past, v_past, h0, conv_prefix = past
    xpad = jnp.concatenate([conv_prefix.astype(xBC.dtype), xBC], axis=1)
    y_ssd, h_last = ssd_mixer(z, xpad, dt_raw, p['conv_w'], p['conv_b'], p['dt_bias'],
                              p['a_log'], p['d_skip'], p['ssd_norm'], h0)
    q = q.reshape(b, L, ATT_HEADS, 2, ATT_HEAD_DIM)
    k = k.reshape(b, L, ATT_HEADS, 2, ATT_HEAD_DIM)
    v = v.reshape(b, L, ATT_HEADS, 2 * ATT_HEAD_DIM)
    lambda_init = 0.8 - 0.6 * math.exp(-0.3 * layer_idx)
    lam = (jnp.exp(jnp.sum(p['lam_q1'].astype(jnp.float32) * p['lam_k1'].astype(jnp.float32)))
           - jnp.exp(jnp.sum(p['lam_q2'].astype(jnp.float32) * p['lam_k2'].astype(jnp.float32)))
           + lambda_init)
    if past is None:
        o = diff_attn_prompt(q, k, v, lam)
    else:
        k_all = jnp.concatenate([k_past.astype(k.dtype), k], axis=1)
        v_all = jnp.concatenate([v_past.astype(v.dtype), v], axis=1)
        o = diff_attn(q, k_all, v_all, lam, None)
    o = (rmsnorm_plain(o) * (1.0 - lambda_init)).reshape(b, L, ATT_WIDTH).astype(x.dtype)
    mix = jnp.concatenate([y_ssd, o], axis=-1) @ p['w_out']
    x = x + g2[:, None, :] * mix
    h = modulate(rmsnorm(x, p['norm3']), sh3, sc3)
    x = x + 0.5 * g3[:, None, :] * swiglu(h, p['ffn2_wgu'], p['ffn2_wd'])
    return x, (k, v, h_last, xpad[:, -(SSD_CONV - 1):])


def setup_inputs(seed: int = 0) -> dict:
    key = jax.random.key(seed)
    ks = iter(jax.random.split(key, 40))
    f32 = jnp.float32

    def nrm(shape, scale):
        return jax.random.normal(next(ks), shape, f32) * scale

    x_prompt = nrm((BATCH, SEQ, D_MODEL), 1.0)
    x_sample = nrm((DEC_BATCH, DEC_SEQ, D_MODEL), 1.0)
    cache_k = nrm((DEPTH, DEC_BATCH, PAST_LEN, ATT_HEADS, 2, ATT_HEAD_DIM), 1.0)
    cache_v = nrm((DEPTH, DEC_BATCH, PAST_LEN, ATT_HEADS, 2 * ATT_HEAD_DIM), 1.0)
    state_ssm = nrm((DEPTH, DEC_BATCH, SSD_HEADS, SSD_HEAD_DIM, SSD_STATE), 0.1)
    state_conv = nrm((DEPTH, DEC_BATCH, SSD_CONV - 1, CONV_CH), 1.0)
    c_prompt = nrm((BATCH, D_MODEL), 1.0)
    c_sample = nrm((DEC_BATCH, D_MODEL), 1.0)
    w_ada = nrm((DEPTH, D_MODEL, N_MOD * D_MODEL), 0.5 * D_MODEL ** -0.5)
    b_ada = nrm((DEPTH, N_MOD * D_MODEL), 0.02)
    norm1 = 1.0 + nrm((DEPTH, D_MODEL), 0.02)
    ffn1_wgu = nrm((DEPTH, D_MODEL, 2 * D_FF), D_MODEL ** -0.5)
    ffn1_wd = nrm((DEPTH, D_FF, D_MODEL), D_FF ** -0.5)
    norm2 = 1.0 + nrm((DEPTH, D_MODEL), 0.02)
    w_in = nrm((DEPTH, D_MODEL, IN_COLS), D_MODEL ** -0.5)
    conv_w = nrm((DEPTH, SSD_CONV, CONV_CH), SSD_CONV ** -0.5)
    conv_b = nrm((DEPTH, CONV_CH), 0.02)
    dt0 = jnp.exp(jax.random.uniform(next(ks), (DEPTH, SSD_HEADS), f32, math.log(1e-3), math.log(1e-1)))
    dt_bias = dt0 + jnp.log(-jnp.expm1(-dt0))
    a_log = jnp.log(jax.random.uniform(next(ks), (DEPTH, SSD_HEADS), f32, 1.0, 16.0))
    d_skip = 1.0 + nrm((DEPTH, SSD_HEADS), 0.1)
    ssd_norm = 1.0 + nrm((DEPTH, SSD_WIDTH), 0.02)
    lam_q1 = nrm((DEPTH, ATT_HEAD_DIM), 0.1)
    lam_k1 = nrm((DEPTH, ATT_HEAD_DIM), 0.1)
    lam_q2 = nrm((DEPTH, ATT_HEAD_DIM), 0.1)
    lam_k2 = nrm((DEPTH, ATT_HEAD_DIM), 0.1)
    w_out = nrm((DEPTH, MIX_WIDTH, D_MODEL), MIX_WIDTH ** -0.5)
    norm3 = 1.0 + nrm((DEPTH, D_MODEL), 0.02)
    ffn2_wgu = nrm((DEPTH, D_MODEL, 2 * D_FF), D_MODEL ** -0.5)
    ffn2_wd = nrm((DEPTH, D_FF, D_MODEL), D_FF ** -0.5)
    final_norm = 1.0 + nrm((D_MODEL,), 0.02)
    return {'x_prompt': x_prompt, 'x_sample': x_sample, 'cache_k': cache_k, 'cache_v': cache_v,
            'state_ssm': state_ssm, 'state_conv': state_conv, 'c_prompt': c_prompt, 'c_sample': c_sample,
            'w_ada': w_ada, 'b_ada': b_ada, 'norm1': norm1, 'ffn1_wgu': ffn1_wgu, 'ffn1_wd': ffn1_wd,
            'norm2': norm2, 'w_in': w_in, 'conv_w': conv_w, 'conv_b': conv_b, 'dt_bias': dt_bias,
            'a_log': a_log, 'd_skip': d_skip, 'ssd_norm': ssd_norm, 'lam_q1': lam_q1, 'lam_k1': lam_k1,
            'lam_q2': lam_q2, 'lam_k2': lam_k2, 'w_out': w_out, 'norm3': norm3, 'ffn2_wgu': ffn2_wgu,
            'ffn2_wd': ffn2_wd, 'final_norm': final_norm}


def reference(x_prompt, x_sample, cache_k, cache_v, state_ssm, state_conv, c_prompt, c_sample,
              w_ada, b_ada, norm1, ffn1_wgu, ffn1_wd, norm2, w_in, conv_w, conv_b, dt_bias,
              a_log, d_skip, ssd_norm, lam_q1, lam_k1, lam_q2, lam_k2, w_out, norm3, ffn2_wgu,
              ffn2_wd, final_norm):
    hp, hs = x_prompt, x_sample
    st_p, st_s = [], []
    for l in range(DEPTH):
        p = {'w_ada': w_ada[l], 'b_ada': b_ada[l], 'norm1': norm1[l], 'ffn1_wgu': ffn1_wgu[l],
             'ffn1_wd': ffn1_wd[l], 'norm2': norm2[l], 'w_in': w_in[l], 'conv_w': conv_w[l],
             'conv_b': conv_b[l], 'dt_bias': dt_bias[l], 'a_log': a_log[l], 'd_skip': d_skip[l],
             'ssd_norm': ssd_norm[l], 'lam_q1': lam_q1[l], 'lam_k1': lam_k1[l], 'lam_q2': lam_q2[l],
             'lam_k2': lam_k2[l], 'w_out': w_out[l], 'norm3': norm3[l], 'ffn2_wgu': ffn2_wgu[l],
             'ffn2_wd': ffn2_wd[l]}
        hp, sp = layer(hp, c_prompt, p, l, None)
        hs, ss = layer(hs, c_sample, p, l, (cache_k[l], cache_v[l], state_ssm[l], state_conv[l]))
        st_p.append(sp)
        st_s.append(ss)
    y_prompt = rmsnorm(hp, final_norm)
    y_sample = rmsnorm(hs, final_norm)
    new_k_prompt = jnp.stack([s[0] for s in st_p])
    new_v_prompt = jnp.stack([s[1] for s in st_p])
    ssm_prompt = jnp.stack([s[2] for s in st_p])
    conv_prompt = jnp.stack([s[3] for s in st_p])
    new_k_sample = jnp.stack([s[0] for s in st_s])
    new_v_sample = jnp.stack([s[1] for s in st_s])
    ssm_sample = jnp.stack([s[2] for s in st_s])
    conv_sample = jnp.stack([s[3] for s in st_s])
    return (y_prompt, y_sample, new_k_prompt, new_v_prompt, ssm_prompt, conv_prompt,
            new_k_sample, new_v_sample, ssm_sample, conv_sample)
```

```python
import numpy as np
import concourse.bass as bass
import concourse.mybir as mybir
from concourse.bass_utils import run_bass_kernel_spmd

F32 = mybir.dt.float32
BF16 = mybir.dt.bfloat16
AF = mybir.ActivationFunctionType
ALU = mybir.AluOpType

D = 1024
DFF = 2816
NPAIR = 22
TT = 512
SEQ = 4096
SS = 64
PAST = 2048
EPS = 1e-6
LAMBDA_INIT = 0.8 - 0.6 * 1.0
NSLOT = 3
SLOT = 4160

BLK_FFN = [4096] * 11 + [2816] * 8
BLK_IN = [4096, 4096, 4160, 4096, 4096, 4096]
BLK_OUT = [4096] * 2
BLKS = BLK_FFN + BLK_IN + BLK_OUT + BLK_FFN
WTOT = sum(BLKS)

V_N1, V_N2, V_N3, V_FN, V_CB, V_CW, V_SN, V_DS, V_DTB, V_ALOG, V_LAM, V_BADA, NVEC = (
    0, 8, 16, 24, 32, 40, 72, 76, 80, 88, 96, 352, 424)
C_ID, C_GT, C_TRI, C_NEG, NCON = 0, 128, 192, 256, 320
NCONB = 768


class Buf:
    __slots__ = ("name", "w", "r", "excl")

    def __init__(self, name, excl=False):
        self.name = name
        self.w = {}
        self.r = {}
        self.excl = excl


def _merge(d, tok):
    k, v = tok
    if d.get(k, 0) < v:
        d[k] = v


class Prog:
    ENGS = ("pe", "act", "dve", "pool", "sp")

    def __init__(self):
        self.ops = {e: [] for e in self.ENGS}
        self.lanes = {}

    def _record(self, tok, reads, writes):
        deps = {}
        R = [b for b in reads if not b.excl]
        W = list(writes) + [b for b in reads if b.excl]
        for b in R:
            for t in b.w.items():
                _merge(deps, t)
        for b in W:
            for t in b.w.items():
                _merge(deps, t)
            for t in b.r.items():
                _merge(deps, t)
        for b in R:
            _merge(b.r, tok)
        for b in W:
            b.w = {tok[0]: tok[1]}
            b.r = {}
        return deps

    def op(self, eng, fn, reads=(), writes=()):
        idx = len(self.ops[eng]) + 1
        tok = (eng, idx)
        deps = self._record(tok, reads, writes)
        self.ops[eng].append(dict(fn=fn, deps=deps, lane=None))

    def dma(self, eng, fn, lane, reads=(), writes=()):
        n = self.lanes.get(lane, 0) + 1
        self.lanes[lane] = n
        tok = ("L:" + lane, n)
        deps = self._record(tok, reads, writes)
        if n > 1:
            _merge(deps, ("L:" + lane, n - 1))
        self.ops[eng].append(dict(fn=fn, deps=deps, lane=lane))

    def group_begin(self, eng):
        self._g = (eng, len(self.ops[eng]), {e: len(self.ops[e]) for e in self.ENGS}, dict(self.lanes))

    def group_end(self):
        eng, start, counts, lanes = self._g
        ops = self.ops[eng][start:]
        if len(ops) > 1:
            merged = {}
            for o in ops:
                for k, v in o["deps"].items():
                    if k == eng:
                        continue
                    if k.startswith("L:"):
                        assert v <= lanes.get(k[2:], 0), "group dep on later dma"
                    else:
                        assert v <= counts[k], "group dep on later op"
                    _merge(merged, (k, v))
            for k, v in ops[0]["deps"].items():
                if k == eng:
                    merged[k] = v
            ops[0]["deps"] = merged

    def guard(self, new_bufs, old_bufs):
        for nb in new_bufs:
            for ob in old_bufs:
                for t in ob.w.items():
                    _merge(nb.r, t)
                for t in ob.r.items():
                    _merge(nb.r, t)

    def emit(self, nc, block, sems, lane_sems):
        marked = {e: set() for e in self.ENGS}
        for e in self.ENGS:
            for o in self.ops[e]:
                for k, v in o["deps"].items():
                    if k.startswith("L:"):
                        continue
                    if k == e and e == "pe":
                        continue
                    marked[k].add(v)
        cnt = {}
        for e in self.ENGS:
            c = 0
            m = {}
            last = 0
            ms = marked[e]
            for i in range(1, len(self.ops[e]) + 1):
                if i in ms:
                    c += 1
                m[i] = c
            cnt[e] = m
        engobj = dict(pe=nc.tensor, act=nc.scalar, dve=nc.vector, pool=nc.gpsimd, sp=nc.sync)
        binder = dict(pe=block.tensor, act=block.scalar, dve=block.vector, pool=block.gpsimd,
                      sp=block.sync)
        final_lanes = dict(self.lanes)

        def make(e):
            def body(eng):
                waited = {}
                ms = marked[e]
                for i, o in enumerate(self.ops[e], start=1):
                    for k, v in o["deps"].items():
                        if k.startswith("L:"):
                            sem = lane_sems[k[2:]]
                            val = 16 * v
                        else:
                            if k == e and e == "pe":
                                continue
                            sem = sems[k]
                            val = cnt[k][v]
                        if waited.get(k, 0) >= val:
                            continue
                        waited[k] = val
                        eng.wait_ge(sem, val)
                    ins = o["fn"](eng)
                    if o["lane"] is not None:
                        ins.then_inc(lane_sems[o["lane"]], 16)
                    elif i in ms:
                        ins.then_inc(sems[e], 1)
                if e == "sp":
                    for ln, n in final_lanes.items():
                        eng.wait_ge(lane_sems[ln], 16 * n)
            return body

        for e in self.ENGS:
            binder[e](make(e))


def build(NT=8, with_sample=True):
    nc = bass.Bass("TRN2", target_bir_lowering=False)
    LP = NT * TT
    dr = lambda name, shape, kind, dt=F32: nc.dram_tensor(name, shape, dt, kind=kind).ap()
    xT = dr("xT", [D, LP], "ExternalInput")
    xsT = dr("xsT", [D, SS], "ExternalInput")
    cT = dr("cT", [128, 16], "ExternalInput")
    wada = dr("wada", [D, 9 * D], "ExternalInput")
    vec_d = dr("vec", [128, NVEC], "ExternalInput")
    con_d = dr("con", [128, NCON], "ExternalInput")
    wall = dr("wall", [128, WTOT], "ExternalInput")
    ckT = dr("ckT", [512, PAST], "ExternalInput")
    cv = dr("cv", [PAST, 512], "ExternalInput")
    sssm = dr("sssm", [128, 512], "ExternalInput")
    sconv = dr("sconv", [128, 24], "ExternalInput")
    scr = dr("scr", [128, WTOT], "Internal", BF16)
    yT = dr("yT", [D, LP], "ExternalOutput")
    ysT = dr("ysT", [D, SS], "ExternalOutput")
    kTo = dr("kTo", [512, LP], "ExternalOutput")
    vo = dr("vo", [LP, 512], "ExternalOutput")
    ksTo = dr("ksTo", [512, SS], "ExternalOutput")
    vso = dr("vso", [SS, 512], "ExternalOutput")
    ssm_o = dr("ssm_o", [128, 512], "ExternalOutput")
    ssms_o = dr("ssms_o", [128, 512], "ExternalOutput")
    conv_o = dr("conv_o", [128, 24], "ExternalOutput")
    convs_o = dr("convs_o", [128, 24], "ExternalOutput")

    P = Prog()
    from contextlib import ExitStack
    es = ExitStack()
    sb = lambda name, shape, dt=F32: es.enter_context(nc.sbuf_tensor(name, shape, dt))
    with es:
        X = sb("X", [128, 8, TT])
        H = sb("H", [128, 8, TT], BF16)
        U = sb("U", [128, 8448])
        RING = sb("RING", [128, NSLOT, SLOT], BF16)
        KT = sb("KT", [128, 4, SEQ], BF16)
        VC = sb("VC", [128, 32, 512], BF16)
        KST = sb("KST", [128, 2, TT])
        VST = sb("VST", [128, 2, 512])
        VEC = sb("VEC", [128, NVEC])
        CON = sb("CON", [128, NCON])
        IDB = sb("IDB", [128, 128], BF16)
        ONEB = sb("ONEB", [128, 128], BF16)
        ONEF = sb("ONEF", [64, 128])
        MHALF = sb("MHALF", [128, 1])
        DG = sb("DG", [128, 32, 128], BF16)
        CTs = sb("CTs", [128, 16])
        SC = sb("SC", [128, 16])
        SCb = sb("SCb", [128, 16], BF16)
        MOD = sb("MOD", [128, 144])
        PV = sb("PV", [128, 2, 6, 8])
        HCB = sb("HCB", [128, 8])
        AN = sb("AN", [128, 8])
        LS = sb("LS", [128, 4])
        NLAM = sb("NLAM", [128, 1])
        S = sb("S", [128, 512])
        SBb = sb("SBb", [128, 512], BF16)
        CVO = sb("CVO", [128, 24])
        CVI = sb("CVI", [128, 24])
        RS = sb("RS", [128, TT])
        LT = RS[:, 0:256]
        TM = sb("TM", [128, 2, TT])
        TM2 = sb("TM2", [128, 2, TT])
        EB = sb("EB", [128, 4, TT], BF16)
        CVP = sb("CVP", [128, 24], BF16)
        SQB = sb("SQB", [128, TT], BF16)
        DTR = sb("DTR", [64, 8, 8])
        DT = sb("DT", [64, 8, 8])
        DA = sb("DA", [64, 8, 8])
        RTH = sb("RTH", [64, 512], BF16)
        RTL = sb("RTL", [64, 512], BF16)
        CONB = sb("CONB", [64, NCONB], BF16)
        DAH = sb("DAH", [64, 8, 8], BF16)
        DAL = sb("DAL", [64, 8, 8], BF16)
        LM = sb("LM", [64, 2, 512])
        EA = sb("EA", [128, 2, 512])
        DD = sb("DD", [64, 2, 8])
        CD = sb("CD", [128, 3, 8])
        CBs = sb("CBs", [64, 2, 128])
        WT = sb("WT", [64, 2, 512], BF16)
        XTk = sb("XTk", [64, 2, 512], BF16)
        XD = sb("XD", [64, 2, 512], BF16)
        BTK = sb("BTK", [64, 2, 256], BF16)
        CP = sb("CP", [128, 2, 512], BF16)
        Ub = U[:].bitcast(BF16)
        A_v = Ub[:, 0:NPAIR * TT].rearrange("p (j t) -> p j t", j=NPAIR)
        o = 0
        ZS = Ub[:, o:o + 4 * TT].rearrange("p (c t) -> p c t", c=4); o += 4 * TT
        XBC = Ub[:, o:o + 8 * 520].rearrange("p (c t) -> p c t", c=8); o += 8 * 520
        XSb = Ub[:, o:o + 4 * TT].rearrange("p (c t) -> p c t", c=4); o += 4 * TT
        BTb = Ub[:, o:o + 2 * TT].rearrange("p (c t) -> p c t", c=2); o += 2 * TT
        CTb = Ub[:, o:o + 2 * TT].rearrange("p (c t) -> p c t", c=2); o += 2 * TT
        QT = Ub[:, o:o + 4 * TT].rearrange("p (c t) -> p c t", c=4); o += 4 * TT
        assert o % 2 == 0
        YS = U[:, o // 2:o // 2 + 4 * TT].rearrange("p (c t) -> p c t", c=4); o += 8 * TT
        assert o <= 16896
        YO = U[:, 0:8 * TT].rearrange("p (c t) -> p c t", c=8)
        WA0 = KT[:, 0:2, :].rearrange("p h t -> p (h t)").rearrange("p (k c) -> p k c", k=8)
        WA1 = KT[:, 2:4, :].rearrange("p h t -> p (h t)").rearrange("p (k c) -> p k c", k=8)
        PSALL = es.enter_context(nc.psum_tensor("psall", [128, 8, 512], F32))
        PS = [PSALL[:, i, :] for i in range(8)]
        bPS = [Buf(f"ps{i}", excl=True) for i in range(8)]
        sems = {e: es.enter_context(nc.semaphore("sem_" + e)) for e in ("pe", "act", "dve", "pool")}
        lane_names = (["c0", "c1", "wa0", "wa1", "cvt0", "cvt1", "cvt2", "cvt3", "x0", "x1", "x2", "x3", "scrw", "yo", "ko", "vo", "misc", "cache"]
                      + [f"r{i}" for i in range(NSLOT)])
        lane_sems = {l: es.enter_context(nc.semaphore("ln_" + l)) for l in lane_names}
        block = es.enter_context(nc.Block())

        bX = [Buf(f"X{c}") for c in range(8)]
        bH = [Buf(f"H{c}") for c in range(8)]
        bA = [Buf(f"A{j}") for j in range(NPAIR)]
        bRING = [Buf(f"ring{i}") for i in range(NSLOT)]
        bKT = [[Buf(f"KT{h}_{t}") for t in range(8)] for h in range(4)]
        bVC = [Buf(f"VC{b}") for b in range(32)]
        bKST = [Buf("KST0"), Buf("KST1")]
        bVST = [Buf("VST0"), Buf("VST1")]
        bZS = [Buf(f"ZS{c}") for c in range(4)]
        bXBC = [Buf(f"XBC{c}") for c in range(8)]
        bXS = [Buf(f"XS{c}") for c in range(4)]
        bBT = [Buf(f"BT{c}") for c in range(2)]
        bCT = [Buf(f"CT{c}") for c in range(2)]
        bQT = [Buf(f"QT{c}") for c in range(4)]
        bYS = [Buf(f"YS{c}") for c in range(4)]
        bYO = [Buf(f"YO{c}") for c in range(8)]
        mixer_bufs = bZS + bXBC + bXS + bBT + bCT + bQT + bYS
        bWA = [Buf("WA0"), Buf("WA1")]
        bSCR = [Buf(f"scr{i}") for i in range(len(BLKS))]
        B = {n: Buf(n) for n in (
            "VEC", "CON", "CONST", "CTs", "SC", "SCb", "MOD", "PV", "AN", "LT", "LS", "NLAM", "S", "SBb",
            "CVO", "CVI", "RS", "TM0", "TM1", "TM20", "TM21", "E0", "E1", "E2", "E3",
            "CVP", "SQB", "DTR", "DT", "DA", "RT", "LM", "EA", "DD", "CD", "CBs", "WT", "XTk", "XD",
            "BTK", "CP", "DG", "RTH", "RTL", "CONB", "DAH", "DAL", "WT1", "XTk1", "XD1", "BTK1", "CP1", "CD1", "LM1", "EA1", "DD1", "CBs1", "CDa", "CDb", "CDc")}
        bTM = [B["TM0"], B["TM1"]]
        bTM2 = [B["TM20"], B["TM21"]]
        bE = [B["E0"], B["E1"], B["E2"], B["E3"]]
        bOA = [bTM[0], bTM[1], bTM2[0]]
        OAv = [TM[:, 0, :], TM[:, 1, :], TM2[:, 0, :]]

        def mm(out, lhsT, rhs, start, stop, reads, bank):
            P.op("pe", lambda e, o=out, l=lhsT, r=rhs, s=start, t=stop:
                 e.matmul(o, lhsT=l, rhs=r, start=s, stop=t, skip_group_check=True),
                 reads=reads, writes=[bank])

        def tr(out, in_, ident, reads, bank):
            P.op("pe", lambda e, o=out, i=in_, d=ident: e.transpose(o, i, d),
                 reads=reads, writes=[bank])

        def act(out, in_, func, reads, writes, bias=None, scale=None):
            kw = {}
            if bias is not None:
                kw["bias"] = bias
            if scale is not None:
                kw["scale"] = scale
            P.op("act", lambda e, o=out, i=in_, f=func, k=kw: e.activation(o, i, f, **k),
                 reads=reads, writes=writes)

        def tt(eng, out, in0, in1, op, reads, writes):
            P.op(eng, lambda e, o=out, a=in0, b=in1, p=op: e.tensor_tensor(out=o, in0=a, in1=b, op=p),
                 reads=reads, writes=writes)

        def ts(eng, out, in0, s1, op0, reads, writes, s2=None, op1=None):
            if op1 is None:
                P.op(eng, lambda e, o=out, a=in0, x=s1, p=op0:
                     e.tensor_scalar(out=o, in0=a, scalar1=x, scalar2=None, op0=p),
                     reads=reads, writes=writes)
            else:
                P.op(eng, lambda e, o=out, a=in0, x=s1, y=s2, p=op0, q=op1:
                     e.tensor_scalar(out=o, in0=a, scalar1=x, scalar2=y, op0=p, op1=q),
                     reads=reads, writes=writes)

        def stt(out, in0, scalar, in1, op0, op1, reads, writes):
            P.op("dve", lambda e, o=out, a=in0, s=scalar, b=in1, p=op0, q=op1:
                 e.scalar_tensor_tensor(out=o, in0=a, scalar=s, in1=b, op0=p, op1=q),
                 reads=reads, writes=writes)

        def cp(eng, out, in_, reads, writes):
            if eng == "act":
                P.op("act", lambda e, o=out, i=in_: e.copy(o, i), reads=reads, writes=writes)
            else:
                P.op(eng, lambda e, o=out, i=in_: e.tensor_copy(out=o, in_=i), reads=reads, writes=writes)

        def dma(eng, out, in_, lane, reads, writes):
            P.dma(eng, lambda e, o=out, i=in_: e.dma_start(out=o, in_=i), lane, reads=reads, writes=writes)

        def mset(eng, ap, val, writes):
            P.op(eng, lambda e, a=ap, v=val: e.memset(a, v), writes=writes)

        dma("sp", VEC[:], vec_d, "c0", [], [B["VEC"]])
        dma("sp", CON[:], con_d, "c1", [], [B["CON"]])
        dma("sp", CTs[:], cT, "c0", [], [B["CTs"]])
        cst = B["CONST"]
        mset("pool", ONEB[:], 1.0, [cst])
        mset("pool", ONEF[:], 1.0, [cst])
        mset("pool", MHALF[:], EPS, [cst])
        cp("dve", IDB[:], CON[:, C_ID:C_ID + 128], [B["CON"]], [cst])
        cp("dve", CONB[:, 0:256], CON[0:64, 0:256], [B["CON"]], [B["CONB"]])
        cp("dve", CONB[:, 256:768].rearrange("p (h l) -> p h l", h=8),
           CON[0:64, C_NEG:C_NEG + 64].unsqueeze(1).to_broadcast([64, 8, 64]), [B["CON"]], [B["CONB"]])
        IDF = CON[:, C_ID:C_ID + 128]
        I64 = CON[0:64, C_ID:C_ID + 64]
        GT = CON[0:64, C_GT:C_GT + 64]
        TRI = CON[0:64, C_TRI:C_TRI + 64]
        for c in range(8):
            for w in range(4):
                col = V_CW + w * 8 + c
                ts("pool" if (c + w) % 2 else "dve", DG[:, c * 4 + w, :], IDF, VEC[:, col:col + 1], ALU.mult,
                   [B["CON"], B["VEC"]], [B["DG"]])
        ts("dve", HCB[:], VEC[:, V_CB:V_CB + 8], 0.5, ALU.mult, [B["VEC"]], [cst])
        act(AN[:], VEC[:, V_ALOG:V_ALOG + 8], AF.Exp, [B["VEC"]], [B["AN"]])
        ts("dve", AN[:], AN[:], -1.0, ALU.mult, [B["AN"]], [B["AN"]])
        tt("dve", LT[:, 0:64], VEC[:, V_LAM:V_LAM + 64], VEC[:, V_LAM + 64:V_LAM + 128], ALU.mult,
           [B["VEC"]], [B["LT"]])
        tt("dve", LT[:, 64:128], VEC[:, V_LAM + 128:V_LAM + 192], VEC[:, V_LAM + 192:V_LAM + 256], ALU.mult,
           [B["VEC"]], [B["LT"]])
        P.op("dve", lambda e: e.reduce_sum(out=LS[:, 0:2], in_=LT[:, 0:128].rearrange("p (a b) -> p a b", a=2),
                                           axis=mybir.AxisListType.X), reads=[B["LT"]], writes=[B["LS"]])
        P.guard([B["RS"]], [B["LT"]])
        act(LS[:, 2:4], LS[:, 0:2], AF.Exp, [B["LS"]], [B["LS"]])
        tt("dve", NLAM[:], LS[:, 3:4], LS[:, 2:3], ALU.subtract, [B["LS"]], [B["NLAM"]])
        ts("dve", NLAM[:], NLAM[:], -LAMBDA_INIT, ALU.add, [B["NLAM"]], [B["NLAM"]])
        act(SC[:], CTs[:], AF.Tanh, [B["CTs"]], [B["SC"]], scale=0.5)
        stt(SC[:], SC[:], 1.0, CTs[:], ALU.add, ALU.mult, [B["SC"], B["CTs"]], [B["SC"]])
        ts("dve", SC[:], SC[:], 0.5, ALU.mult, [B["SC"]], [B["SC"]])
        cp("dve", SCb[:], SC[:], [B["SC"]], [B["SCb"]])
        wada_v = wada.rearrange("(k p) n -> p k n", p=128)
        WAs = [WA0, WA1]
        for i in range(9):
            s = i % 2
            dma("pool", WAs[s], wada_v[:, :, i * D:(i + 1) * D], "wa%d" % s, [], [bWA[s]])
            for jj in range(8):
                j = i * 8 + jj
                for k in range(8):
                    mm(PS[7][:, 2 * j:2 * j + 2], WAs[s][:, k, jj * 128:(jj + 1) * 128],
                       SCb[:].rearrange("p (k s) -> p k s", s=2)[:, k, :], k == 0, k == 7,
                       [bWA[s], B["SCb"]], bPS[7])
        tt("dve", MOD[:].rearrange("p (j s) -> p j s", s=2), PS[7][:, 0:144].rearrange("p (j s) -> p j s", s=2),
           VEC[:, V_BADA:V_BADA + 72].unsqueeze(2).to_broadcast([128, 72, 2]), ALU.add,
           [bPS[7], B["VEC"]], [B["MOD"]])
        MODv = MOD[:].rearrange("p (j s) -> p j s", s=2)
        for s in range(2):
            for n in range(3):
                sc_j = 8 * (3 * n + 1)
                stt(PV[:, s, n, :], MODv[:, sc_j:sc_j + 8, s], 1.0, VEC[:, 8 * n:8 * n + 8], ALU.add, ALU.mult,
                    [B["MOD"], B["VEC"]], [B["PV"]])
                g_j = 8 * (3 * n + 2)
                ts("dve", PV[:, s, 3 + n, :], MODv[:, g_j:g_j + 8, s], 1.0 if n == 1 else 0.25, ALU.mult,
                   [B["MOD"]], [B["PV"]])

        def shv(s, n, c):
            j = 8 * 3 * n + c
            return MOD[:, 2 * j + s:2 * j + s + 1]

        P.guard([b for row in bKT for b in row] + bVC, bWA)

        offs = np.concatenate([[0], np.cumsum(BLKS)]).astype(int)

        class WS:
            issued = 0
            cur = 0
            total = 0
        nblk = len(BLKS)

        def w_issue():
            g = WS.issued
            i = g % nblk
            s = g % NSLOT
            n = BLKS[i]
            if g < nblk:
                dma("pool", RING[:, s, 0:n], wall[:, offs[i]:offs[i] + n], "cvt%d" % (g % 4), [], [bRING[s]])
                dma("sp", scr[:, offs[i]:offs[i] + n], RING[:, s, 0:n], "scrw", [bRING[s]], [bSCR[i]])
            else:
                dma("sp", RING[:, s, 0:n], scr[:, offs[i]:offs[i] + n], "r%d" % s, [bSCR[i]], [bRING[s]])
            WS.issued += 1

        def w_next():
            g = WS.cur
            while WS.issued < min(WS.total, g + NSLOT - 1 + 1):
                w_issue()
            WS.cur += 1
            s = g % NSLOT
            assert g < WS.issued
            return RING[:, s, :], bRING[s], BLKS[g % nblk]

        ntiles = NT + (1 if with_sample else 0)
        WS.total = ntiles * nblk

        def rms_to_H(T, s, n):
            for c in range(8):
                act(H[:, c, 0:T], X[:, c, 0:T], AF.Square, [bX[c]], [bH[c]])
                mm(PS[7][:, 0:T], ONEB[:], H[:, c, 0:T], c == 0, c == 7, [B["CONST"], bH[c]], bPS[7])
            act(RS[:, 0:T], PS[7][:, 0:T], AF.Ln, [bPS[7], B["CONST"]], [B["RS"]], bias=MHALF[:, 0:1], scale=1.0 / D)
            act(RS[:, 0:T], RS[:, 0:T], AF.Exp, [B["RS"]], [B["RS"]], scale=-0.5)
            for c in range(8):
                k = c % 2
                tt("dve", TM[:, k, 0:T], X[:, c, 0:T], RS[:, 0:T], ALU.mult, [bX[c], B["RS"]], [bTM[k]])
                act(H[:, c, 0:T], TM[:, k, 0:T], AF.Identity, [bTM[k], B["PV"], B["MOD"]], [bH[c]],
                    bias=shv(s, n, c), scale=PV[:, s, n, c:c + 1])

        def ffn(T, s, gate_n):
            for jb in range(11):
                wv, wb, _ = w_next()
                for jj in range(2):
                    j = 2 * jb + jj
                    pg, pu = (0, 1) if j % 2 == 0 else (2, 3)
                    base = jj * 2048
                    if j > 0:
                        P.group_begin("pe")
                    for k in range(8):
                        mm(PS[pg][:, 0:T], wv[:, base + k * 128:base + (k + 1) * 128], H[:, k, 0:T],
                           k == 0, k == 7, [wb, bH[k]], bPS[pg])
                    for k in range(8):
                        mm(PS[pu][:, 0:T], wv[:, base + 1024 + k * 128:base + 1024 + (k + 1) * 128], H[:, k, 0:T],
                           k == 0, k == 7, [wb, bH[k]], bPS[pu])
                    if j > 0:
                        P.group_end()
                    q = j % 2
                    act(TM[:, q, 0:T], PS[pg][:, 0:T], AF.Tanh, [bPS[pg]], [bTM[q]], scale=0.5)
                    stt(TM[:, q, 0:T], TM[:, q, 0:T], 1.0, PS[pg][:, 0:T], ALU.add, ALU.mult,
                        [bTM[q], bPS[pg]], [bTM[q]])
                    tt("dve", A_v[:, j, 0:T], TM[:, q, 0:T], PS[pu][:, 0:T], ALU.mult, [bTM[q], bPS[pu]], [bA[j]])
            for m in range(8):
                wv, wb, _ = w_next()
                pb = 4 + m % 2
                if m > 0:
                    P.group_begin("pe")
                for k in range(NPAIR):
                    mm(PS[pb][:, 0:T], wv[:, k * 128:(k + 1) * 128], A_v[:, k, 0:T], k == 0, k == NPAIR - 1,
                       [wb, bA[k]], bPS[pb])
                if m > 0:
                    P.group_end()
                stt(X[:, m, 0:T], PS[pb][:, 0:T], PV[:, s, 3 + gate_n, m:m + 1], X[:, m, 0:T], ALU.mult, ALU.add,
                    [bPS[pb], B["PV"], bX[m]], [bX[m]])

        def silu_from_psum(T, ps, bank, dst, dstbuf, q, hb=None):
            if hb is None:
                act(TM[:, q, 0:T], ps, AF.Tanh, [bank], [bTM[q]], scale=0.5)
                act(TM2[:, q, 0:T], ps, AF.Identity, [bank], [bTM2[q]], scale=0.5)
            else:
                act(TM[:, q, 0:T], ps, AF.Tanh, [bank, B["CONST"]], [bTM[q]], scale=0.5, bias=hb)
                act(TM2[:, q, 0:T], ps, AF.Identity, [bank, B["CONST"]], [bTM2[q]], scale=0.5, bias=hb)
            stt(dst, TM[:, q, 0:T], 1.0, TM2[:, q, 0:T], ALU.add, ALU.mult, [bTM[q], bTM2[q]], [dstbuf])

        def in_proj(T, s, t0, tile_idx, is_last, kout, vout):
            nch = T // 64
            rot = [0]

            def nb():
                b = rot[0] % 4
                rot[0] += 1
                return b

            def fm_block(kind):
                wv, wb, _ = w_next()
                for cc in range(4):
                    b = nb()
                    for k in range(8):
                        mm(PS[b][:, 0:T], wv[:, cc * 1024 + k * 128:cc * 1024 + (k + 1) * 128], H[:, k, 0:T],
                           k == 0, k == 7, [wb, bH[k]], bPS[b])
                    src = PS[b][:, 0:T]
                    if kind == "k":
                        h = cc
                        kb = bKT[h][tile_idx]
                        cp("act", KT[:, h, t0:t0 + T], src, [bPS[b]], [kb])
                        q = h % 2
                        cp("dve", KST[:, q, 0:T], src, [bPS[b]], [bKST[q]])
                        dma("sp", kout[h * 128:(h + 1) * 128, t0:t0 + T], KST[:, q, 0:T], "ko", [bKST[q]], [])
                    elif kind == "q":
                        cp("act", QT[:, cc, 0:T], src, [bPS[b]], [bQT[cc]])
                    elif kind in ("x0", "x1"):
                        c = (0 if kind == "x0" else 4) + cc
                        cp("act", XBC[:, c, 3:3 + T], src, [bPS[b]], [bXBC[c]])
                        if is_last:
                            cp("dve", CVO[:, c * 3:c * 3 + 3], PS[b][:, T - 3:T], [bPS[b]], [B["CVO"]])
                    else:
                        silu_from_psum(T, src, bPS[b], ZS[:, cc, 0:T], bZS[cc], cc % 2)

            fm_block("x0")
            fm_block("x1")
            for c in range(8):
                cp("dve", XBC[:, c, 0:3], CVP[:, c * 3:c * 3 + 3], [B["CVP"]], [bXBC[c]])
            conv(T)
            if not is_last:
                for c in range(8):
                    cp("dve", CVP[:, c * 3:c * 3 + 3], XBC[:, c, T:T + 3], [bXBC[c]], [B["CVP"]])
            wv, wb, _ = w_next()
            b = nb()
            for ch in range(nch):
                for k in range(8):
                    mm(PS[b][0:64, ch * 8:ch * 8 + 8], H[:, k, ch * 64:(ch + 1) * 64],
                       wv[:, 4096 + k * 8:4096 + (k + 1) * 8], k == 0, k == 7, [wb, bH[k]], bPS[b])
            tt("dve", DTR[:, 0:nch, :], PS[b][0:64, 0:nch * 8].rearrange("p (c h) -> p c h", h=8),
               VEC[0:64, V_DTB:V_DTB + 8].unsqueeze(1).to_broadcast([64, nch, 8]), ALU.add,
               [bPS[b], B["VEC"]], [B["DTR"]])
            act(DTR[:, 0:nch, :], DTR[:, 0:nch, :], AF.Exp, [B["DTR"]], [B["DTR"]])
            act(DT[:, 0:nch, :], DTR[:, 0:nch, :], AF.Ln, [B["DTR"]], [B["DT"]], bias=1.0)
            tt("dve", DA[:, 0:nch, :], DT[:, 0:nch, :], AN[0:64, :].unsqueeze(1).to_broadcast([64, nch, 8]), ALU.mult,
               [B["DT"], B["AN"]], [B["DA"]])
            cp("dve", DAH[:, 0:nch, :], DA[:, 0:nch, :], [B["DA"]], [B["DAH"]])
            tt("dve", DAL[:, 0:nch, :], DA[:, 0:nch, :], DAH[:, 0:nch, :], ALU.subtract, [B["DA"], B["DAH"]], [B["DAL"]])
            ntb = (T + 127) // 128
            for tb in range(ntb):
                nt_ = min(128, T - tb * 128)
                b = nb()
                for k in range(8):
                    mm(PS[b][0:nt_, :], H[:, k, tb * 128:tb * 128 + nt_], wv[:, k * 512:(k + 1) * 512],
                       k == 0, k == 7, [wb, bH[k]], bPS[b])
                blkid = (t0 // 128) + tb
                cp("act", VC[0:nt_, blkid, :], PS[b][0:nt_, :], [bPS[b]], [bVC[blkid]])
                q = tb % 2
                cp("dve", VST[0:nt_, q, :], PS[b][0:nt_, :], [bPS[b]], [bVST[q]])
                dma("sp", vout[t0 + tb * 128:t0 + tb * 128 + nt_, :], VST[0:nt_, q, :], "vo", [bVST[q]], [])
            fm_block("k")
            fm_block("q")
            fm_block("z")

        def conv(T):
            for c in range(8):
                b = 4 + c % 2
                for w in range(4):
                    mm(PS[b][:, 0:T], DG[:, c * 4 + w, :], XBC[:, c, w:w + T], w == 0, w == 3,
                       [B["DG"], bXBC[c]], bPS[b])
                if c < 4:
                    dst, db = XSb[:, c, 0:T], bXS[c]
                elif c < 6:
                    dst, db = BTb[:, c - 4, 0:T], bBT[c - 4]
                else:
                    dst, db = CTb[:, c - 6, 0:T], bCT[c - 6]
                silu_from_psum(T, PS[b][:, 0:T], bPS[b], dst, db, c % 2, hb=HCB[:, c:c + 1])

        GTb = CONB[:, C_GT:C_GT + 64]
        I64b = CONB[:, C_ID:C_ID + 64]
        NEGb = CONB[:, C_NEG:C_NEG + 512]

        def pb(name, q):
            return B[name if q == 0 else name + "1"]

        def ssd_1(ch):
            q = ch % 2
            cs = slice(ch * 64, (ch + 1) * 64)
            dA = DA[:, ch, :]
            tt("dve", RTH[:].rearrange("p (h l) -> p h l", h=8), TRI.unsqueeze(1).to_broadcast([64, 8, 64]),
               DAH[:, ch, :].unsqueeze(2).to_broadcast([64, 8, 64]), ALU.mult, [B["CON"], B["DAH"]], [B["RTH"]])
            tt("dve", RTL[:].rearrange("p (h l) -> p h l", h=8), TRI.unsqueeze(1).to_broadcast([64, 8, 64]),
               DAL[:, ch, :].unsqueeze(2).to_broadcast([64, 8, 64]), ALU.mult, [B["CON"], B["DAL"]], [B["RTL"]])
            P.group_begin("pe")
            mm(PS[0][0:64, :], GTb, RTH[:], True, False, [B["CONB"], B["RTH"]], bPS[0])
            mm(PS[0][0:64, :], GTb, RTL[:], False, True, [B["CONB"], B["RTL"]], bPS[0])
            mm(PS[1][:, :], ONEB[0:64, :], RTH[:], True, False, [B["CONST"], B["RTH"]], bPS[1])
            mm(PS[1][:, :], ONEB[0:64, :], RTL[:], False, True, [B["CONST"], B["RTL"]], bPS[1])
            mm(PS[2][0:64, 0:8], GT, dA, True, True, [B["CON"], B["DA"]], bPS[2])
            mm(PS[2][:, 8:16], ONEF[:], dA, True, True, [B["CONST"], B["DA"]], bPS[2])
            for g in range(2):
                mm(PS[2][0:64, 64 + g * 64:128 + g * 64], BTb[:, g, cs], CTb[:, g, cs], True, True,
                   [bBT[g], bCT[g]], bPS[2])
            P.group_end()
            act(LM[:, q, :], PS[0][0:64, :], AF.Exp, [bPS[0]], [pb("LM", q)])
            act(EA[:, q, :], PS[1][:, :], AF.Exp, [bPS[1]], [pb("EA", q)])
            act(DD[:, q, :], PS[2][0:64, 0:8], AF.Exp, [bPS[2]], [pb("DD", q)])
            act(CD[:, ch % 3, :], PS[2][:, 8:16], AF.Exp, [bPS[2]], [B["CD" + "abc"[ch % 3]]])
            tt("dve", CBs[:, q, :].rearrange("p (g l) -> p g l", g=2),
               PS[2][0:64, 64:192].rearrange("p (g l) -> p g l", g=2),
               TRI.unsqueeze(1).to_broadcast([64, 2, 64]), ALU.mult, [bPS[2], B["CON"]], [pb("CBs", q)])

        def ssd_2a(ch):
            q = ch % 2
            cs = slice(ch * 64, (ch + 1) * 64)
            ps3 = PS[3][:].bitcast(BF16)
            P.group_begin("pe")
            for c in range(4):
                tr(ps3[0:64, c * 128:(c + 1) * 128], XSb[:, c, cs], IDB[:], [bXS[c], B["CONST"]], bPS[3])
            for g in range(2):
                tr(ps3[0:64, 512 + g * 128:512 + (g + 1) * 128], BTb[:, g, cs], IDB[:], [bBT[g], B["CONST"]], bPS[3])
            P.group_end()
            tt("dve", DD[:, q, :], DD[:, q, :], DT[:, ch, :], ALU.mult, [pb("DD", q), B["DT"]], [pb("DD", q)])
            tt("dve", XD[:, q, :].rearrange("p (h l) -> p h l", h=8), ps3[0:64, 0:512].rearrange("p (h l) -> p h l", h=8),
               DD[:, q, :].unsqueeze(2).to_broadcast([64, 8, 64]), ALU.mult, [bPS[3], pb("DD", q)], [pb("XD", q)])
            cp("act", XTk[:, q, :], ps3[0:64, 0:512], [bPS[3]], [pb("XTk", q)])
            cp("act", BTK[:, q, :], ps3[0:64, 512:768], [bPS[3]], [pb("BTK", q)])

        def ssd_2b(ch):
            q = ch % 2
            cs = slice(ch * 64, (ch + 1) * 64)
            tt("dve", LM[:, q, :].rearrange("p (g r l) -> p g r l", g=2, r=4),
               LM[:, q, :].rearrange("p (g r l) -> p g r l", g=2, r=4),
               CBs[:, q, :].rearrange("p (g l) -> p g l", g=2).unsqueeze(2).to_broadcast([64, 2, 4, 64]), ALU.mult,
               [pb("LM", q), pb("CBs", q)], [pb("LM", q)])
            tt("dve", WT[:, q, :].rearrange("p (h l) -> p h l", h=8), LM[:, q, :].rearrange("p (h l) -> p h l", h=8),
               DT[:, ch, :].unsqueeze(2).to_broadcast([64, 8, 64]), ALU.mult, [pb("LM", q), B["DT"]], [pb("WT", q)])
            tt("dve", CP[:, q, :].rearrange("p (g r l) -> p g r l", g=2, r=4),
               EA[:, q, :].rearrange("p (g r l) -> p g r l", g=2, r=4),
               CTb[:, :, cs].unsqueeze(2).to_broadcast([128, 2, 4, 64]), ALU.mult, [pb("EA", q), bCT[0], bCT[1]], [pb("CP", q)])

        def ssd_3(ch):
            q = ch % 2
            cs = slice(ch * 64, (ch + 1) * 64)
            P.group_begin("pe")
            for g in range(2):
                mm(PS[5][:, g * 256:(g + 1) * 256], BTK[:, q, g * 128:(g + 1) * 128], XD[:, q, g * 256:(g + 1) * 256],
                   True, True, [pb("BTK", q), pb("XD", q)], bPS[5])
            P.group_end()
            P.group_begin("pe")
            for h in range(8):
                po = (h % 2) * 64
                out = PS[4][po:po + 64, (h // 2) * 64:(h // 2) * 64 + 64]
                mm(out, XTk[:, q, h * 64:(h + 1) * 64], WT[:, q, h * 64:(h + 1) * 64], True, False,
                   [pb("XTk", q), pb("WT", q)], bPS[4])
                mm(out, SBb[:, h * 64:(h + 1) * 64], CP[:, q, h * 64:(h + 1) * 64], False, True,
                   [B["SBb"], pb("CP", q)], bPS[4])
            P.group_end()
            tt("dve", S[:].rearrange("p (h l) -> p h l", h=8), S[:].rearrange("p (h l) -> p h l", h=8),
               CD[:, ch % 3, :].unsqueeze(2).to_broadcast([128, 8, 64]), ALU.mult, [B["S"], B["CD" + "abc"[ch % 3]]], [B["S"]])
            tt("dve", S[:], S[:], PS[5][:, :], ALU.add, [B["S"], bPS[5]], [B["S"]])
            cp("act", SBb[:], S[:], [B["S"]], [B["SBb"]])
            cp("act", YS[:, :, cs], PS[4][:, 0:256].rearrange("p (c l) -> p c l", c=4), [bPS[4]], bYS)

        def ssd_post(T):
            for c in range(4):
                stt(YS[:, c, 0:T], XSb[:, c, 0:T], VEC[:, V_DS + c:V_DS + c + 1], YS[:, c, 0:T], ALU.mult, ALU.add,
                    [bXS[c], B["VEC"], bYS[c]], [bYS[c]])
                tt("dve", YS[:, c, 0:T], YS[:, c, 0:T], ZS[:, c, 0:T], ALU.mult, [bYS[c], bZS[c]], [bYS[c]])
                k = c % 2
                act(EB[:, k, 0:T], YS[:, c, 0:T], AF.Square, [bYS[c]], [bE[k]])
                mm(PS[7][:, 0:T], ONEB[:], EB[:, k, 0:T], c == 0, c == 3, [B["CONST"], bE[k]], bPS[7])
            act(RS[:, 0:T], PS[7][:, 0:T], AF.Ln, [bPS[7], B["CONST"]], [B["RS"]], bias=MHALF[:, 0:1], scale=1.0 / 512)
            act(RS[:, 0:T], RS[:, 0:T], AF.Exp, [B["RS"]], [B["RS"]], scale=-0.5)
            for c in range(4):
                stt(H[:, c, 0:T], YS[:, c, 0:T], VEC[:, V_SN + c:V_SN + c + 1], RS[:, 0:T], ALU.mult, ALU.mult,
                    [bYS[c], B["VEC"], B["RS"]], [bH[c]])

        def attention(T, kblocks):
            nb_ = len(kblocks)
            pending = [None]
            for h in range(4):
                def emit_S(bi, h=h):
                    kc0, nk, vblk, pbase, q0, ktile = kblocks[bi]
                    P.group_begin("pe")
                    for j in range(2):
                        sb_ = (bi % 2) * 2 + j
                        mm(PS[sb_][pbase:pbase + nk, q0:T], KT[j * 64:(j + 1) * 64, h, kc0:kc0 + nk],
                           QT[j * 64:(j + 1) * 64, h, q0:T], True, True, [bKT[h][ktile], bQT[h]], bPS[sb_])
                    P.group_end()
                    s0 = (bi % 2) * 2
                    act(EB[pbase:pbase + nk, s0:s0 + 2, q0:T], PSALL[pbase:pbase + nk, s0:s0 + 2, q0:T], AF.Exp,
                        [bPS[s0], bPS[s0 + 1]], [bE[s0], bE[s0 + 1]], scale=0.125)

                def emit_AV(bi, h=h):
                    kc0, nk, vblk, pbase, q0, ktile = kblocks[bi]
                    first, last = bi == 0, bi == nb_ - 1
                    P.group_begin("pe")
                    for j in range(2):
                        e_i = (bi % 2) * 2 + j
                        mm(PS[4 + j][:, q0:T], VC[pbase:pbase + nk, vblk, h * 128:(h + 1) * 128],
                           EB[pbase:pbase + nk, e_i, q0:T], first, last, [bVC[vblk], bE[e_i]], bPS[4 + j])
                    for j in range(2):
                        e_i = (bi % 2) * 2 + j
                        mm(PS[6 + j][:, q0:T], ONEB[pbase:pbase + nk, :], EB[pbase:pbase + nk, e_i, q0:T],
                           first, last, [B["CONST"], bE[e_i]], bPS[6 + j])
                    P.group_end()

                for i in range(nb_ + 1):
                    if i < nb_:
                        emit_S(i)
                    if i >= 1:
                        emit_AV(i - 1)
                    if i == 1 and pending[0] is not None:
                        pending[0]()
                        pending[0] = None
                for j in range(2):
                    act(OAv[j][:, 0:T], PS[6 + j][:, 0:T], AF.Ln, [bPS[6 + j]], [bOA[j]])
                for j in range(2):
                    act(OAv[j][:, 0:T], OAv[j][:, 0:T], AF.Exp, [bOA[j]], [bOA[j]], scale=-1.0)
                    tt("dve", OAv[j][:, 0:T], PS[4 + j][:, 0:T], OAv[j][:, 0:T], ALU.mult, [bPS[4 + j], bOA[j]], [bOA[j]])
                stt(OAv[2][:, 0:T], OAv[1][:, 0:T], NLAM[:, 0:1], OAv[0][:, 0:T], ALU.mult, ALU.add,
                    [bOA[0], bOA[1], B["NLAM"]], [bOA[2]])

                def part2(h=h):
                    act(SQB[:, 0:T], OAv[2][:, 0:T], AF.Square, [bOA[2]], [B["SQB"]])
                    mm(PS[0][:, 0:T], ONEB[:], SQB[:, 0:T], True, True, [B["CONST"], B["SQB"]], bPS[0])
                    act(RS[:, 0:T], PS[0][:, 0:T], AF.Ln, [bPS[0], B["CONST"]], [B["RS"]], bias=MHALF[:, 0:1], scale=1.0 / 128)
                    act(RS[:, 0:T], RS[:, 0:T], AF.Exp, [B["RS"]], [B["RS"]], scale=-0.5)
                    stt(H[:, 4 + h, 0:T], OAv[2][:, 0:T], 1.0 - LAMBDA_INIT, RS[:, 0:T], ALU.mult, ALU.mult,
                        [bOA[2], B["RS"]], [bH[4 + h]])
                pending[0] = part2
            pending[0]()

        def out_proj(T, s):
            for blk in range(2):
                wv, wb, _ = w_next()
                for cc in range(4):
                    m = blk * 4 + cc
                    b = m % 2
                    for k in range(8):
                        mm(PS[b][:, 0:T], wv[:, cc * 1024 + k * 128:cc * 1024 + (k + 1) * 128], H[:, k, 0:T],
                           k == 0, k == 7, [wb, bH[k]], bPS[b])
                    stt(X[:, m, 0:T], PS[b][:, 0:T], PV[:, s, 4, m:m + 1], X[:, m, 0:T], ALU.mult, ALU.add,
                        [bPS[b], B["PV"], bX[m]], [bX[m]])

        def final_norm(T, yout, t0):
            for c in range(8):
                k = c % 2
                act(EB[:, k, 0:T], X[:, c, 0:T], AF.Square, [bX[c]], [bE[k]])
                mm(PS[7][:, 0:T], ONEB[:], EB[:, k, 0:T], c == 0, c == 7, [B["CONST"], bE[k]], bPS[7])
            act(RS[:, 0:T], PS[7][:, 0:T], AF.Ln, [bPS[7], B["CONST"]], [B["RS"]], bias=MHALF[:, 0:1], scale=1.0 / D)
            act(RS[:, 0:T], RS[:, 0:T], AF.Exp, [B["RS"]], [B["RS"]], scale=-0.5)
            P.guard(bYO, bA + mixer_bufs)
            for c in range(8):
                stt(YO[:, c, 0:T], X[:, c, 0:T], VEC[:, V_FN + c:V_FN + c + 1], RS[:, 0:T], ALU.mult, ALU.mult,
                    [bX[c], B["VEC"], B["RS"]], [bYO[c]])
            dma("sp", yout.rearrange("(c p) t -> p c t", p=128)[:, :, t0:t0 + T], YO[:, :, 0:T], "yo", bYO, [])
            P.guard(bA + mixer_bufs, bYO)

        def tile(T, s, xin, t0, tile_idx, is_last, kblocks, yout, kout, vout):
            xin_v = xin.rearrange("(c p) t -> p c t", p=128)
            for i4 in range(4):
                dma("sp", X[:, 2 * i4:2 * i4 + 2, 0:T], xin_v[:, 2 * i4:2 * i4 + 2, t0:t0 + T], "x%d" % i4, [],
                    bX[2 * i4:2 * i4 + 2])
            rms_to_H(T, s, 0)
            P.guard(bA, mixer_bufs)
            ffn(T, s, 0)
            rms_to_H(T, s, 1)
            P.guard(mixer_bufs, bA)
            in_proj(T, s, t0, tile_idx, is_last, kout, vout)
            nch_ = T // 64
            for i in range(nch_ + 2):
                if 1 <= i <= nch_:
                    ssd_2a(i - 1)
                if i >= 2:
                    ssd_3(i - 2)
                if i < nch_:
                    ssd_1(i)
                if 1 <= i <= nch_:
                    ssd_2b(i - 1)
            ssd_post(T)
            attention(T, kblocks)
            out_proj(T, s)
            rms_to_H(T, s, 2)
            P.guard(bA, mixer_bufs)
            ffn(T, s, 2)
            final_norm(T, yout, t0)

        mset("dve", S[:], 0.0, [B["S"]])
        mset("dve", SBb[:], 0.0, [B["SBb"]])
        mset("dve", CVP[:], 0.0, [B["CVP"]])
        for t in range(NT):
            kbl = [(128 * kb, 128, kb, 0, 0, kb // 4) for kb in range(4 * t)]
            kbl += [(TT * t + 64 * c, 64, 4 * t + c // 2, (c % 2) * 64, 64 * c, t) for c in range(8)]
            tile(TT, 0, xT, t * TT, t, t == NT - 1, kbl, yT, kTo, vo)
        dma("sp", ssm_o, S[:], "misc", [B["S"]], [])
        dma("sp", conv_o, CVO[:], "misc", [B["CVO"]], [])

        if with_sample:
            for h in range(4):
                dma("pool", KT[:, h, 2048:4096], ckT[h * 128:(h + 1) * 128, :], "cache", [], bKT[h][4:8])
            for i in range(4):
                dma("pool", VC[:, 16 + 4 * i:20 + 4 * i, :],
                    cv[i * 512:(i + 1) * 512, :].rearrange("(b p) e -> p b e", p=128), "cache", [],
                    bVC[16 + 4 * i:20 + 4 * i])
            dma("sp", S[:], sssm, "misc", [], [B["S"]])
            cp("act", SBb[:], S[:], [B["S"]], [B["SBb"]])
            dma("sp", CVI[:], sconv, "misc", [], [B["CVI"]])
            cp("dve", CVP[:], CVI[:], [B["CVI"]], [B["CVP"]])
            kbl = [(2048 + 128 * i, 128, 16 + i, 0, 0, 4 + i // 4) for i in range(16)] + [(0, 64, 0, 0, 0, 0)]
            tile(SS, 1, xsT, 0, 0, True, kbl, ysT, ksTo, vso)
            dma("sp", ssms_o, S[:], "misc", [B["S"]], [])
            dma("sp", convs_o, CVO[:], "misc", [B["CVO"]], [])

        P.emit(nc, block, sems, lane_sems)
    return nc


def _chunk128(v):
    return np.ascontiguousarray(v.reshape(-1, 128).T)


def _lhs_chunks(w):
    K, M = w.shape
    out = []
    for m in range(M // 128):
        blk = w[:, m * 128:(m + 1) * 128].reshape(K // 128, 128, 128).transpose(1, 0, 2).reshape(128, -1)
        out.append(blk)
    return out


def _build_wall(ffn1_wgu, ffn1_wd, w_in, w_out, ffn2_wgu, ffn2_wd):
    def ffn_blocks(wgu, wd):
        g = _lhs_chunks(wgu[:, :DFF])
        u = _lhs_chunks(wgu[:, DFF:])
        blks = []
        for jb in range(11):
            blks.append(np.concatenate([g[2 * jb], u[2 * jb], g[2 * jb + 1], u[2 * jb + 1]], axis=1))
        blks += _lhs_chunks(wd)
        return blks
    z, xbc, dt, q, k, v = (w_in[:, 0:512], w_in[:, 512:1536], w_in[:, 1536:1544], w_in[:, 1544:2056],
                           w_in[:, 2056:2568], w_in[:, 2568:3080])
    fm = _lhs_chunks(xbc) + _lhs_chunks(k) + _lhs_chunks(q) + _lhs_chunks(z)
    fmb = [np.concatenate(fm[4 * i:4 * i + 4], axis=1) for i in range(5)]
    vb = v.reshape(8, 128, 512).transpose(1, 0, 2).reshape(128, -1)
    dtb = dt.reshape(8, 128, 8).transpose(1, 0, 2).reshape(128, -1)
    inb = [fmb[0], fmb[1], np.concatenate([vb, dtb], axis=1), fmb[2], fmb[3], fmb[4]]
    om = _lhs_chunks(w_out)
    outb = [np.concatenate(om[0:4], axis=1), np.concatenate(om[4:8], axis=1)]
    allb = ffn_blocks(ffn1_wgu, ffn1_wd) + inb + outb + ffn_blocks(ffn2_wgu, ffn2_wd)
    assert [b.shape[1] for b in allb] == BLKS
    return np.ascontiguousarray(np.concatenate(allb, axis=1), dtype=np.float32)


def _consts():
    con = np.zeros((128, NCON), np.float32)
    con[:, C_ID:C_ID + 128] = np.eye(128, dtype=np.float32)
    t = np.arange(64)
    con[0:64, C_GT:C_GT + 64] = (t[:, None] > t[None, :]).astype(np.float32)
    con[0:64, C_TRI:C_TRI + 64] = (t[:, None] <= t[None, :]).astype(np.float32)
    neg = np.where(t[:, None] > t[None, :], -30000.0, 0.0).astype(np.float32)
    con[0:64, C_NEG:C_NEG + 64] = neg
    return con


_NC_CACHE = {}


def _prep(inputs, NT):
    f = lambda a: np.asarray(a, dtype=np.float32)
    wall = _build_wall(f(inputs["ffn1_wgu"])[0], f(inputs["ffn1_wd"])[0], f(inputs["w_in"])[0],
                       f(inputs["w_out"])[0], f(inputs["ffn2_wgu"])[0], f(inputs["ffn2_wd"])[0])
    con = _consts()
    vec = np.zeros((128, NVEC), np.float32)
    vec[:, V_N1:V_N1 + 8] = _chunk128(f(inputs["norm1"])[0])
    vec[:, V_N2:V_N2 + 8] = _chunk128(f(inputs["norm2"])[0])
    vec[:, V_N3:V_N3 + 8] = _chunk128(f(inputs["norm3"])[0])
    vec[:, V_FN:V_FN + 8] = _chunk128(f(inputs["final_norm"]))
    vec[:, V_CB:V_CB + 8] = _chunk128(f(inputs["conv_b"])[0])
    cw = f(inputs["conv_w"])[0]
    for w in range(4):
        vec[:, V_CW + w * 8:V_CW + w * 8 + 8] = _chunk128(cw[w])
    vec[:, V_SN:V_SN + 4] = _chunk128(f(inputs["ssd_norm"])[0])
    vec[:, V_DS:V_DS + 4] = _chunk128(np.repeat(f(inputs["d_skip"])[0], 64))
    vec[:, V_DTB:V_DTB + 8] = f(inputs["dt_bias"])[0][None, :]
    vec[:, V_ALOG:V_ALOG + 8] = f(inputs["a_log"])[0][None, :]
    lam = np.concatenate([f(inputs[n])[0] for n in ("lam_q1", "lam_k1", "lam_q2", "lam_k2")])
    vec[:, V_LAM:V_LAM + 256] = lam[None, :]
    vec[:, V_BADA:V_BADA + 72] = _chunk128(f(inputs["b_ada"])[0])
    wada = np.ascontiguousarray(f(inputs["w_ada"])[0])
    xp, xs = f(inputs["x_prompt"]), f(inputs["x_sample"])
    cp_, cs_ = f(inputs["c_prompt"]), f(inputs["c_sample"])
    ck, cvv = f(inputs["cache_k"])[0], f(inputs["cache_v"])[0]
    st, sc = f(inputs["state_ssm"])[0], f(inputs["state_conv"])[0]
    maps = []
    for b in range(8):
        cT = np.stack([_chunk128(cp_[b]), _chunk128(cs_[b])], axis=2).reshape(128, 16)
        maps.append({
            "xT": np.ascontiguousarray(xp[b, :NT * TT].T),
            "xsT": np.ascontiguousarray(xs[b].T),
            "cT": np.ascontiguousarray(cT),
            "wada": wada, "vec": vec, "con": con, "wall": wall,
            "ckT": np.ascontiguousarray(ck[b].reshape(PAST, 512).T),
            "cv": np.ascontiguousarray(cvv[b].reshape(PAST, 512)),
            "sssm": np.ascontiguousarray(st[b].reshape(512, 128).T),
            "sconv": np.ascontiguousarray(sc[b].reshape(3, 8, 128).transpose(2, 1, 0).reshape(128, 24)),
        })
    return maps


def _gather(res, NT):
    L = NT * TT
    g = lambda n: [np.asarray(r[n], dtype=np.float32) for r in res]
    y = np.stack([a.T for a in g("yT")])
    ys = np.stack([a.T for a in g("ysT")])
    nk = np.stack([a.T.reshape(L, 4, 2, 64) for a in g("kTo")])[None]
    nv = np.stack([a.reshape(L, 4, 128) for a in g("vo")])[None]
    ssm = np.stack([a.T.reshape(8, 64, 128) for a in g("ssm_o")])[None]
    conv = np.stack([a.reshape(128, 8, 3).transpose(2, 1, 0).reshape(3, 1024) for a in g("conv_o")])[None]
    nks = np.stack([a.T.reshape(SS, 4, 2, 64) for a in g("ksTo")])[None]
    nvs = np.stack([a.reshape(SS, 4, 128) for a in g("vso")])[None]
    ssms = np.stack([a.T.reshape(8, 64, 128) for a in g("ssms_o")])[None]
    convs = np.stack([a.reshape(128, 8, 3).transpose(2, 1, 0).reshape(3, 1024) for a in g("convs_o")])[None]
    return tuple(np.ascontiguousarray(a, dtype=np.float32) for a in
                 (y, ys, nk, nv, ssm, conv, nks, nvs, ssms, convs))


def run(inputs, NT=8, cores=8):
    maps = _prep(inputs, NT)[:cores]
    nc = build(NT)
    res = run_bass_kernel_spmd(nc, maps, core_ids=list(range(cores)))
    return _gather(res.results, NT)


def kernel(**inputs):
    return run(inputs, 8, 8)
```

```python
import numpy as np
import concourse.bass as bass
import concourse.mybir as mybir
from concourse.bass_utils import run_bass_kernel_spmd

F32 = mybir.dt.float32
BF16 = mybir.dt.bfloat16
AF = mybir.ActivationFunctionType
ALU = mybir.AluOpType

D = 1024
DFF = 2816
NPAIR = 22
TT = 512
SEQ = 4096
SS = 64
PAST = 2048
EPS = 1e-6
LAMBDA_INIT = 0.8 - 0.6 * 1.0
NSLOT = 3
SLOT = 4160

BLK_FFN = [4096] * 11 + [2816] * 8
BLK_IN = [4096, 4096, 4160, 4096, 4096, 4096]
BLK_OUT = [4096] * 2
BLKS = BLK_FFN + BLK_IN + BLK_OUT + BLK_FFN
WTOT = sum(BLKS)

V_N1, V_N2, V_N3, V_FN, V_CB, V_CW, V_SN, V_DS, V_DTB, V_ALOG, V_LAM, V_BADA, NVEC = (
    0, 8, 16, 24, 32, 40, 72, 76, 80, 88, 96, 352, 424)
C_ID, C_GT, C_TRI, C_NEG, NCON = 0, 128, 192, 256, 320
NCONB = 768


class Buf:
    __slots__ = ("name", "w", "r", "excl")

    def __init__(self, name, excl=False):
        self.name = name
        self.w = {}
        self.r = {}
        self.excl = excl


def _merge(d, tok):
    k, v = tok
    if d.get(k, 0) < v:
        d[k] = v


class Prog:
    ENGS = ("pe", "act", "dve", "pool", "sp")

    def __init__(self):
        self.ops = {e: [] for e in self.ENGS}
        self.lanes = {}

    def _record(self, tok, reads, writes):
        deps = {}
        R = [b for b in reads if not b.excl]
        W = list(writes) + [b for b in reads if b.excl]
        for b in R:
            for t in b.w.items():
                _merge(deps, t)
        for b in W:
            for t in b.w.items():
                _merge(deps, t)
            for t in b.r.items():
                _merge(deps, t)
        for b in R:
            _merge(b.r, tok)
        for b in W:
            b.w = {tok[0]: tok[1]}
            b.r = {}
        return deps

    def op(self, eng, fn, reads=(), writes=()):
        idx = len(self.ops[eng]) + 1
        tok = (eng, idx)
        deps = self._record(tok, reads, writes)
        self.ops[eng].append(dict(fn=fn, deps=deps, lane=None))

    def dma(self, eng, fn, lane, reads=(), writes=()):
        n = self.lanes.get(lane, 0) + 1
        self.lanes[lane] = n
        tok = ("L:" + lane, n)
        deps = self._record(tok, reads, writes)
        if n > 1:
            _merge(deps, ("L:" + lane, n - 1))
        self.ops[eng].append(dict(fn=fn, deps=deps, lane=lane))

    def group_begin(self, eng):
        self._g = (eng, len(self.ops[eng]), {e: len(self.ops[e]) for e in self.ENGS}, dict(self.lanes))

    def group_end(self):
        eng, start, counts, lanes = self._g
        ops = self.ops[eng][start:]
        if len(ops) > 1:
            merged = {}
            for o in ops:
                for k, v in o["deps"].items():
                    if k == eng:
                        continue
                    if k.startswith("L:"):
                        assert v <= lanes.get(k[2:], 0), "group dep on later dma"
                    else:
                        assert v <= counts[k], "group dep on later op"
                    _merge(merged, (k, v))
            for k, v in ops[0]["deps"].items():
                if k == eng:
                    merged[k] = v
            ops[0]["deps"] = merged

    def guard(self, new_bufs, old_bufs):
        for nb in new_bufs:
            for ob in old_bufs:
                for t in ob.w.items():
                    _merge(nb.r, t)
                for t in ob.r.items():
                    _merge(nb.r, t)

    def emit(self, nc, block, sems, lane_sems):
        marked = {e: set() for e in self.ENGS}
        for e in self.ENGS:
            for o in self.ops[e]:
                for k, v in o["deps"].items():
                    if k.startswith("L:"):
                        continue
                    if k == e and e == "pe":
                        continue
                    marked[k].add(v)
        cnt = {}
        for e in self.ENGS:
            c = 0
            m = {}
            last = 0
            ms = marked[e]
            for i in range(1, len(self.ops[e]) + 1):
                if i in ms:
                    c += 1
                m[i] = c
            cnt[e] = m
        engobj = dict(pe=nc.tensor, act=nc.scalar, dve=nc.vector, pool=nc.gpsimd, sp=nc.sync)
        binder = dict(pe=block.tensor, act=block.scalar, dve=block.vector, pool=block.gpsimd,
                      sp=block.sync)
        final_lanes = dict(self.lanes)

        def make(e):
            def body(eng):
                waited = {}
                ms = marked[e]
                for i, o in enumerate(self.ops[e], start=1):
                    for k, v in o["deps"].items():
                        if k.startswith("L:"):
                            sem = lane_sems[k[2:]]
                            val = 16 * v
                        else:
                            if k == e and e == "pe":
                                continue
                            sem = sems[k]
                            val = cnt[k][v]
                        if waited.get(k, 0) >= val:
                            continue
                        waited[k] = val
                        eng.wait_ge(sem, val)
                    ins = o["fn"](eng)
                    if o["lane"] is not None:
                        ins.then_inc(lane_sems[o["lane"]], 16)
                    elif i in ms:
                        ins.then_inc(sems[e], 1)
                if e == "sp":
                    for ln, n in final_lanes.items():
                        eng.wait_ge(lane_sems[ln], 16 * n)
            return body

        for e in self.ENGS:
            binder[e](make(e))


def build(NT=8, with_sample=True):
    nc = bass.Bass("TRN2", target_bir_lowering=False)
    LP = NT * TT
    dr = lambda name, shape, kind, dt=F32: nc.dram_tensor(name, shape, dt, kind=kind).ap()
    xT = dr("xT", [D, LP], "ExternalInput")
    xsT = dr("xsT", [D, SS], "ExternalInput")
    cT = dr("cT", [128, 16], "ExternalInput")
    wada = dr("wada", [D, 9 * D], "ExternalInput")
    vec_d = dr("vec", [128, NVEC], "ExternalInput")
    con_d = dr("con", [128, NCON], "ExternalInput")
    wall = dr("wall", [128, WTOT], "ExternalInput")
    ckT = dr("ckT", [512, PAST], "ExternalInput")
    cv = dr("cv", [PAST, 512], "ExternalInput")
    sssm = dr("sssm", [128, 512], "ExternalInput")
    sconv = dr("sconv", [128, 24], "ExternalInput")
    scr = dr("scr", [128, WTOT], "Internal", BF16)
    yT = dr("yT", [D, LP], "ExternalOutput")
    ysT = dr("ysT", [D, SS], "ExternalOutput")
    kTo = dr("kTo", [512, LP], "ExternalOutput")
    vo = dr("vo", [LP, 512], "ExternalOutput")
    ksTo = dr("ksTo", [512, SS], "ExternalOutput")
    vso = dr("vso", [SS, 512], "ExternalOutput")
    ssm_o = dr("ssm_o", [128, 512], "ExternalOutput")
    ssms_o = dr("ssms_o", [128, 512], "ExternalOutput")
    conv_o = dr("conv_o", [128, 24], "ExternalOutput")
    convs_o = dr("convs_o", [128, 24], "ExternalOutput")

    P = Prog()
    from contextlib import ExitStack
    es = ExitStack()
    sb = lambda name, shape, dt=F32: es.enter_context(nc.sbuf_tensor(name, shape, dt))
    with es:
        X = sb("X", [128, 8, TT])
        H = sb("H", [128, 8, TT], BF16)
        U = sb("U", [128, 8448])
        RING = sb("RING", [128, NSLOT, SLOT], BF16)
        KT = sb("KT", [128, 4, SEQ], BF16)
        VC = sb("VC", [128, 32, 512], BF16)
        KST = sb("KST", [128, 2, TT])
        VST = sb("VST", [128, 2, 512])
        VEC = sb("VEC", [128, NVEC])
        CON = sb("CON", [128, NCON])
        IDB = sb("IDB", [128, 128], BF16)
        ONEB = sb("ONEB", [128, 128], BF16)
        ONEF = sb("ONEF", [64, 128])
        MHALF = sb("MHALF", [128, 1])
        DG = sb("DG", [128, 32, 128], BF16)
        CTs = sb("CTs", [128, 16])
        SC = sb("SC", [128, 16])
        SCb = sb("SCb", [128, 16], BF16)
        MOD = sb("MOD", [128, 144])
        PV = sb("PV", [128, 2, 6, 8])
        HCB = sb("HCB", [128, 8])
        AN = sb("AN", [128, 8])
        LS = sb("LS", [128, 4])
        NLAM = sb("NLAM", [128, 1])
        S = sb("S", [128, 512])
        SBb = sb("SBb", [128, 512], BF16)
        CVO = sb("CVO", [128, 24])
        CVI = sb("CVI", [128, 24])
        RS = sb("RS", [128, TT])
        LT = RS[:, 0:256]
        TM = sb("TM", [128, 2, TT])
        TM2 = sb("TM2", [128, 2, TT])
        EB = sb("EB", [128, 4, TT], BF16)
        CVP = sb("CVP", [128, 24], BF16)
        SQB = sb("SQB", [128, TT], BF16)
        DTR = sb("DTR", [64, 8, 8])
        DT = sb("DT", [64, 8, 8])
        DA = sb("DA", [64, 8, 8])
        RTH = sb("RTH", [64, 512], BF16)
        RTL = sb("RTL", [64, 512], BF16)
        CONB = sb("CONB", [64, NCONB], BF16)
        DAH = sb("DAH", [64, 8, 8], BF16)
        DAL = sb("DAL", [64, 8, 8], BF16)
        LM = sb("LM", [64, 2, 512])
        EA = sb("EA", [128, 2, 512])
        DD = sb("DD", [64, 2, 8])
        CD = sb("CD", [128, 3, 8])
        CBs = sb("CBs", [64, 2, 128])
        WT = sb("WT", [64, 2, 512], BF16)
        XTk = sb("XTk", [64, 2, 512], BF16)
        XD = sb("XD", [64, 2, 512], BF16)
        BTK = sb("BTK", [64, 2, 256], BF16)
        CP = sb("CP", [128, 2, 512], BF16)
        Ub = U[:].bitcast(BF16)
        A_v = Ub[:, 0:NPAIR * TT].rearrange("p (j t) -> p j t", j=NPAIR)
        o = 0
        ZS = Ub[:, o:o + 4 * TT].rearrange("p (c t) -> p c t", c=4); o += 4 * TT
        XBC = Ub[:, o:o + 8 * 520].rearrange("p (c t) -> p c t", c=8); o += 8 * 520
        XSb = Ub[:, o:o + 4 * TT].rearrange("p (c t) -> p c t", c=4); o += 4 * TT
        BTb = Ub[:, o:o + 2 * TT].rearrange("p (c t) -> p c t", c=2); o += 2 * TT
        CTb = Ub[:, o:o + 2 * TT].rearrange("p (c t) -> p c t", c=2); o += 2 * TT
        QT = Ub[:, o:o + 4 * TT].rearrange("p (c t) -> p c t", c=4); o += 4 * TT
        assert o % 2 == 0
        YS = U[:, o // 2:o // 2 + 4 * TT].rearrange("p (c t) -> p c t", c=4); o += 8 * TT
        assert o <= 16896
        YO = U[:, 0:8 * TT].rearrange("p (c t) -> p c t", c=8)
        WA0 = KT[:, 0:2, :].rearrange("p h t -> p (h t)").rearrange("p (k c) -> p k c", k=8)
        WA1 = KT[:, 2:4, :].rearrange("p h t -> p (h t)").rearrange("p (k c) -> p k c", k=8)
        PSALL = es.enter_context(nc.psum_tensor("psall", [128, 8, 512], F32))
        PS = [PSALL[:, i, :] for i in range(8)]
        bPS = [Buf(f"ps{i}", excl=True) for i in range(8)]
        sems = {e: es.enter_context(nc.semaphore("sem_" + e)) for e in ("pe", "act", "dve", "pool")}
        lane_names = (["c0", "c1", "wa0", "wa1", "cvt0", "cvt1", "cvt2", "cvt3", "x0", "x1", "x2", "x3", "scrw", "yo", "ko", "vo", "misc", "cache"]
                      + [f"r{i}" for i in range(NSLOT)])
        lane_sems = {l: es.enter_context(nc.semaphore("ln_" + l)) for l in lane_names}
        block = es.enter_context(nc.Block())

        bX = [Buf(f"X{c}") for c in range(8)]
        bH = [Buf(f"H{c}") for c in range(8)]
        bA = [Buf(f"A{j}") for j in range(NPAIR)]
        bRING = [Buf(f"ring{i}") for i in range(NSLOT)]
        bKT = [[Buf(f"KT{h}_{t}") for t in range(8)] for h in range(4)]
        bVC = [Buf(f"VC{b}") for b in range(32)]
        bKST = [Buf("KST0"), Buf("KST1")]
        bVST = [Buf("VST0"), Buf("VST1")]
        bZS = [Buf(f"ZS{c}") for c in range(4)]
        bXBC = [Buf(f"XBC{c}") for c in range(8)]
        bXS = [Buf(f"XS{c}") for c in range(4)]
        bBT = [Buf(f"BT{c}") for c in range(2)]
        bCT = [Buf(f"CT{c}") for c in range(2)]
        bQT = [Buf(f"QT{c}") for c in range(4)]
        bYS = [Buf(f"YS{c}") for c in range(4)]
        bYO = [Buf(f"YO{c}") for c in range(8)]
        mixer_bufs = bZS + bXBC + bXS + bBT + bCT + bQT + bYS
        bWA = [Buf("WA0"), Buf("WA1")]
        bSCR = [Buf(f"scr{i}") for i in range(len(BLKS))]
        B = {n: Buf(n) for n in (
            "VEC", "CON", "CONST", "CTs", "SC", "SCb", "MOD", "PV", "AN", "LT", "LS", "NLAM", "S", "SBb",
            "CVO", "CVI", "RS", "TM0", "TM1", "TM20", "TM21", "E0", "E1", "E2", "E3",
            "CVP", "SQB", "DTR", "DT", "DA", "RT", "LM", "EA", "DD", "CD", "CBs", "WT", "XTk", "XD",
            "BTK", "CP", "DG", "RTH", "RTL", "CONB", "DAH", "DAL", "WT1", "XTk1", "XD1", "BTK1", "CP1", "CD1", "LM1", "EA1", "DD1", "CBs1", "CDa", "CDb", "CDc")}
        bTM = [B["TM0"], B["TM1"]]
        bTM2 = [B["TM20"], B["TM21"]]
        bE = [B["E0"], B["E1"], B["E2"], B["E3"]]
        bOA = [bTM[0], bTM[1], bTM2[0]]
        OAv = [TM[:, 0, :], TM[:, 1, :], TM2[:, 0, :]]

        def mm(out, lhsT, rhs, start, stop, reads, bank):
            P.op("pe", lambda e, o=out, l=lhsT, r=rhs, s=start, t=stop:
                 e.matmul(o, lhsT=l, rhs=r, start=s, stop=t, skip_group_check=True),
                 reads=reads, writes=[bank])

        def tr(out, in_, ident, reads, bank):
            P.op("pe", lambda e, o=out, i=in_, d=ident: e.transpose(o, i, d),
                 reads=reads, writes=[bank])

        def act(out, in_, func, reads, writes, bias=None, scale=None):
            kw = {}
            if bias is not None:
                kw["bias"] = bias
            if scale is not None:
                kw["scale"] = scale
            P.op("act", lambda e, o=out, i=in_, f=func, k=kw: e.activation(o, i, f, **k),
                 reads=reads, writes=writes)

        def tt(eng, out, in0, in1, op, reads, writes):
            P.op(eng, lambda e, o=out, a=in0, b=in1, p=op: e.tensor_tensor(out=o, in0=a, in1=b, op=p),
                 reads=reads, writes=writes)

        def ts(eng, out, in0, s1, op0, reads, writes, s2=None, op1=None):
            if op1 is None:
                P.op(eng, lambda e, o=out, a=in0, x=s1, p=op0:
                     e.tensor_scalar(out=o, in0=a, scalar1=x, scalar2=None, op0=p),
                     reads=reads, writes=writes)
            else:
                P.op(eng, lambda e, o=out, a=in0, x=s1, y=s2, p=op0, q=op1:
                     e.tensor_scalar(out=o, in0=a, scalar1=x, scalar2=y, op0=p, op1=q),
                     reads=reads, writes=writes)

        def stt(out, in0, scalar, in1, op0, op1, reads, writes):
            P.op("dve", lambda e, o=out, a=in0, s=scalar, b=in1, p=op0, q=op1:
                 e.scalar_tensor_tensor(out=o, in0=a, scalar=s, in1=b, op0=p, op1=q),
                 reads=reads, writes=writes)

        def cp(eng, out, in_, reads, writes):
            if eng == "act":
                P.op("act", lambda e, o=out, i=in_: e.copy(o, i), reads=reads, writes=writes)
            else:
                P.op(eng, lambda e, o=out, i=in_: e.tensor_copy(out=o, in_=i), reads=reads, writes=writes)

        def dma(eng, out, in_, lane, reads, writes):
            P.dma(eng, lambda e, o=out, i=in_: e.dma_start(out=o, in_=i), lane, reads=reads, writes=writes)

        def mset(eng, ap, val, writes):
            P.op(eng, lambda e, a=ap, v=val: e.memset(a, v), writes=writes)

        dma("sp", VEC[:], vec_d, "c0", [], [B["VEC"]])
        dma("sp", CON[:], con_d, "c1", [], [B["CON"]])
        dma("sp", CTs[:], cT, "c0", [], [B["CTs"]])
        cst = B["CONST"]
        mset("pool", ONEB[:], 1.0, [cst])
        mset("pool", ONEF[:], 1.0, [cst])
        mset("pool", MHALF[:], EPS, [cst])
        cp("dve", IDB[:], CON[:, C_ID:C_ID + 128], [B["CON"]], [cst])
        cp("dve", CONB[:, 0:256], CON[0:64, 0:256], [B["CON"]], [B["CONB"]])
        cp("dve", CONB[:, 256:768].rearrange("p (h l) -> p h l", h=8),
           CON[0:64, C_NEG:C_NEG + 64].unsqueeze(1).to_broadcast([64, 8, 64]), [B["CON"]], [B["CONB"]])
        IDF = CON[:, C_ID:C_ID + 128]
        I64 = CON[0:64, C_ID:C_ID + 64]
        GT = CON[0:64, C_GT:C_GT + 64]
        TRI = CON[0:64, C_TRI:C_TRI + 64]
        for c in range(8):
            for w in range(4):
                col = V_CW + w * 8 + c
                ts("pool" if (c + w) % 2 else "dve", DG[:, c * 4 + w, :], IDF, VEC[:, col:col + 1], ALU.mult,
                   [B["CON"], B["VEC"]], [B["DG"]])
        ts("dve", HCB[:], VEC[:, V_CB:V_CB + 8], 0.5, ALU.mult, [B["VEC"]], [cst])
        act(AN[:], VEC[:, V_ALOG:V_ALOG + 8], AF.Exp, [B["VEC"]], [B["AN"]])
        ts("dve", AN[:], AN[:], -1.0, ALU.mult, [B["AN"]], [B["AN"]])
        tt("dve", LT[:, 0:64], VEC[:, V_LAM:V_LAM + 64], VEC[:, V_LAM + 64:V_LAM + 128], ALU.mult,
           [B["VEC"]], [B["LT"]])
        tt("dve", LT[:, 64:128], VEC[:, V_LAM + 128:V_LAM + 192], VEC[:, V_LAM + 192:V_LAM + 256], ALU.mult,
           [B["VEC"]], [B["LT"]])
        P.op("dve", lambda e: e.reduce_sum(out=LS[:, 0:2], in_=LT[:, 0:128].rearrange("p (a b) -> p a b", a=2),
                                           axis=mybir.AxisListType.X), reads=[B["LT"]], writes=[B["LS"]])
        P.guard([B["RS"]], [B["LT"]])
        act(LS[:, 2:4], LS[:, 0:2], AF.Exp, [B["LS"]], [B["LS"]])
        tt("dve", NLAM[:], LS[:, 3:4], LS[:, 2:3], ALU.subtract, [B["LS"]], [B["NLAM"]])
        ts("dve", NLAM[:], NLAM[:], -LAMBDA_INIT, ALU.add, [B["NLAM"]], [B["NLAM"]])
        act(SC[:], CTs[:], AF.Tanh, [B["CTs"]], [B["SC"]], scale=0.5)
        stt(SC[:], SC[:], 1.0, CTs[:], ALU.add, ALU.mult, [B["SC"], B["CTs"]], [B["SC"]])
        ts("dve", SC[:], SC[:], 0.5, ALU.mult, [B["SC"]], [B["SC"]])
        cp("dve", SCb[:], SC[:], [B["SC"]], [B["SCb"]])
        wada_v = wada.rearrange("(k p) n -> p k n", p=128)
        WAs = [WA0, WA1]
        for i in range(9):
            s = i % 2
            dma("pool", WAs[s], wada_v[:, :, i * D:(i + 1) * D], "wa%d" % s, [], [bWA[s]])
            for jj in range(8):
                j = i * 8 + jj
                for k in range(8):
                    mm(PS[7][:, 2 * j:2 * j + 2], WAs[s][:, k, jj * 128:(jj + 1) * 128],
                       SCb[:].rearrange("p (k s) -> p k s", s=2)[:, k, :], k == 0, k == 7,
                       [bWA[s], B["SCb"]], bPS[7])
        tt("dve", MOD[:].rearrange("p (j s) -> p j s", s=2), PS[7][:, 0:144].rearrange("p (j s) -> p j s", s=2),
           VEC[:, V_BADA:V_BADA + 72].unsqueeze(2).to_broadcast([128, 72, 2]), ALU.add,
           [bPS[7], B["VEC"]], [B["MOD"]])
        MODv = MOD[:].rearrange("p (j s) -> p j s", s=2)
        for s in range(2):
            for n in range(3):
                sc_j = 8 * (3 * n + 1)
                stt(PV[:, s, n, :], MODv[:, sc_j:sc_j + 8, s], 1.0, VEC[:, 8 * n:8 * n + 8], ALU.add, ALU.mult,
                    [B["MOD"], B["VEC"]], [B["PV"]])
                g_j = 8 * (3 * n + 2)
                ts("dve", PV[:, s, 3 + n, :], MODv[:, g_j:g_j + 8, s], 1.0 if n == 1 else 0.25, ALU.mult,
                   [B["MOD"]], [B["PV"]])

        def shv(s, n, c):
            j = 8 * 3 * n + c
            return MOD[:, 2 * j + s:2 * j + s + 1]

        P.guard([b for row in bKT for b in row] + bVC, bWA)

        offs = np.concatenate([[0], np.cumsum(BLKS)]).astype(int)

        class WS:
            issued = 0
            cur = 0
            total = 0
        nblk = len(BLKS)

        def w_issue():
            g = WS.issued
            i = g % nblk
            s = g % NSLOT
            n = BLKS[i]
            if g < nblk:
                dma("pool", RING[:, s, 0:n], wall[:, offs[i]:offs[i] + n], "cvt%d" % (g % 4), [], [bRING[s]])
                dma("sp", scr[:, offs[i]:offs[i] + n], RING[:, s, 0:n], "scrw", [bRING[s]], [bSCR[i]])
            else:
                dma("sp", RING[:, s, 0:n], scr[:, offs[i]:offs[i] + n], "r%d" % s, [bSCR[i]], [bRING[s]])
            WS.issued += 1

        def w_next():
            g = WS.cur
            while WS.issued < min(WS.total, g + NSLOT - 1 + 1):
                w_issue()
            WS.cur += 1
            s = g % NSLOT
            assert g < WS.issued
            return RING[:, s, :], bRING[s], BLKS[g % nblk]

        ntiles = NT + (1 if with_sample else 0)
        WS.total = ntiles * nblk

        def rms_to_H(T, s, n):
            for c in range(8):
                act(H[:, c, 0:T], X[:, c, 0:T], AF.Square, [bX[c]], [bH[c]])
                mm(PS[7][:, 0:T], ONEB[:], H[:, c, 0:T], c == 0, c == 7, [B["CONST"], bH[c]], bPS[7])
            act(RS[:, 0:T], PS[7][:, 0:T], AF.Ln, [bPS[7], B["CONST"]], [B["RS"]], bias=MHALF[:, 0:1], scale=1.0 / D)
            act(RS[:, 0:T], RS[:, 0:T], AF.Exp, [B["RS"]], [B["RS"]], scale=-0.5)
            for c in range(8):
                k = c % 2
                tt("dve", TM[:, k, 0:T], X[:, c, 0:T], RS[:, 0:T], ALU.mult, [bX[c], B["RS"]], [bTM[k]])
                act(H[:, c, 0:T], TM[:, k, 0:T], AF.Identity, [bTM[k], B["PV"], B["MOD"]], [bH[c]],
                    bias=shv(s, n, c), scale=PV[:, s, n, c:c + 1])

        def ffn(T, s, gate_n):
            for jb in range(11):
                wv, wb, _ = w_next()
                for jj in range(2):
                    j = 2 * jb + jj
                    pg, pu = (0, 1) if j % 2 == 0 else (2, 3)
                    base = jj * 2048
                    if j > 0:
                        P.group_begin("pe")
                    for k in range(8):
                        mm(PS[pg][:, 0:T], wv[:, base + k * 128:base + (k + 1) * 128], H[:, k, 0:T],
                           k == 0, k == 7, [wb, bH[k]], bPS[pg])
                    for k in range(8):
                        mm(PS[pu][:, 0:T], wv[:, base + 1024 + k * 128:base + 1024 + (k + 1) * 128], H[:, k, 0:T],
                           k == 0, k == 7, [wb, bH[k]], bPS[pu])
                    if j > 0:
                        P.group_end()
                    q = j % 2
                    act(TM[:, q, 0:T], PS[pg][:, 0:T], AF.Tanh, [bPS[pg]], [bTM[q]], scale=0.5)
                    stt(TM[:, q, 0:T], TM[:, q, 0:T], 1.0, PS[pg][:, 0:T], ALU.add, ALU.mult,
                        [bTM[q], bPS[pg]], [bTM[q]])
                    tt("dve", A_v[:, j, 0:T], TM[:, q, 0:T], PS[pu][:, 0:T], ALU.mult, [bTM[q], bPS[pu]], [bA[j]])
            for m in range(8):
                wv, wb, _ = w_next()
                pb = 4 + m % 2
                if m > 0:
                    P.group_begin("pe")
                for k in range(NPAIR):
                    mm(PS[pb][:, 0:T], wv[:, k * 128:(k + 1) * 128], A_v[:, k, 0:T], k == 0, k == NPAIR - 1,
                       [wb, bA[k]], bPS[pb])
                if m > 0:
                    P.group_end()
                stt(X[:, m, 0:T], PS[pb][:, 0:T], PV[:, s, 3 + gate_n, m:m + 1], X[:, m, 0:T], ALU.mult, ALU.add,
                    [bPS[pb], B["PV"], bX[m]], [bX[m]])

        def silu_from_psum(T, ps, bank, dst, dstbuf, q, hb=None):
            if hb is None:
                act(TM[:, q, 0:T], ps, AF.Tanh, [bank], [bTM[q]], scale=0.5)
                act(TM2[:, q, 0:T], ps, AF.Identity, [bank], [bTM2[q]], scale=0.5)
            else:
                act(TM[:, q, 0:T], ps, AF.Tanh, [bank, B["CONST"]], [bTM[q]], scale=0.5, bias=hb)
                act(TM2[:, q, 0:T], ps, AF.Identity, [bank, B["CONST"]], [bTM2[q]], scale=0.5, bias=hb)
            stt(dst, TM[:, q, 0:T], 1.0, TM2[:, q, 0:T], ALU.add, ALU.mult, [bTM[q], bTM2[q]], [dstbuf])

        def in_proj(T, s, t0, tile_idx, is_last, kout, vout):
            nch = T // 64
            rot = [0]

            def nb():
                b = rot[0] % 4
                rot[0] += 1
                return b

            def fm_block(kind):
                wv, wb, _ = w_next()
                for cc in range(4):
                    b = nb()
                    for k in range(8):
                        mm(PS[b][:, 0:T], wv[:, cc * 1024 + k * 128:cc * 1024 + (k + 1) * 128], H[:, k, 0:T],
                           k == 0, k == 7, [wb, bH[k]], bPS[b])
                    src = PS[b][:, 0:T]
                    if kind == "k":
                        h = cc
                        kb = bKT[h][tile_idx]
                        cp("act", KT[:, h, t0:t0 + T], src, [bPS[b]], [kb])
                        q = h % 2
                        cp("dve", KST[:, q, 0:T], src, [bPS[b]], [bKST[q]])
                        dma("sp", kout[h * 128:(h + 1) * 128, t0:t0 + T], KST[:, q, 0:T], "ko", [bKST[q]], [])
                    elif kind == "q":
                        cp("act", QT[:, cc, 0:T], src, [bPS[b]], [bQT[cc]])
                    elif kind in ("x0", "x1"):
                        c = (0 if kind == "x0" else 4) + cc
                        cp("act", XBC[:, c, 3:3 + T], src, [bPS[b]], [bXBC[c]])
                        if is_last:
                            cp("dve", CVO[:, c * 3:c * 3 + 3], PS[b][:, T - 3:T], [bPS[b]], [B["CVO"]])
                    else:
                        silu_from_psum(T, src, bPS[b], ZS[:, cc, 0:T], bZS[cc], cc % 2)

            fm_block("x0")
            fm_block("x1")
            for c in range(8):
                cp("dve", XBC[:, c, 0:3], CVP[:, c * 3:c * 3 + 3], [B["CVP"]], [bXBC[c]])
            conv(T)
            if not is_last:
                for c in range(8):
                    cp("dve", CVP[:, c * 3:c * 3 + 3], XBC[:, c, T:T + 3], [bXBC[c]], [B["CVP"]])
            wv, wb, _ = w_next()
            b = nb()
            for ch in range(nch):
                for k in range(8):
                    mm(PS[b][0:64, ch * 8:ch * 8 + 8], H[:, k, ch * 64:(ch + 1) * 64],
                       wv[:, 4096 + k * 8:4096 + (k + 1) * 8], k == 0, k == 7, [wb, bH[k]], bPS[b])
            tt("dve", DTR[:, 0:nch, :], PS[b][0:64, 0:nch * 8].rearrange("p (c h) -> p c h", h=8),
               VEC[0:64, V_DTB:V_DTB + 8].unsqueeze(1).to_broadcast([64, nch, 8]), ALU.add,
               [bPS[b], B["VEC"]], [B["DTR"]])
            act(DTR[:, 0:nch, :], DTR[:, 0:nch, :], AF.Exp, [B["DTR"]], [B["DTR"]])
            act(DT[:, 0:nch, :], DTR[:, 0:nch, :], AF.Ln, [B["DTR"]], [B["DT"]], bias=1.0)
            tt("dve", DA[:, 0:nch, :], DT[:, 0:nch, :], AN[0:64, :].unsqueeze(1).to_broadcast([64, nch, 8]), ALU.mult,
               [B["DT"], B["AN"]], [B["DA"]])
            cp("dve", DAH[:, 0:nch, :], DA[:, 0:nch, :], [B["DA"]], [B["DAH"]])
            tt("dve", DAL[:, 0:nch, :], DA[:, 0:nch, :], DAH[:, 0:nch, :], ALU.subtract, [B["DA"], B["DAH"]], [B["DAL"]])
            ntb = (T + 127) // 128
            for tb in range(ntb):
                nt_ = min(128, T - tb * 128)
                b = nb()
                for k in range(8):
                    mm(PS[b][0:nt_, :], H[:, k, tb * 128:tb * 128 + nt_], wv[:, k * 512:(k + 1) * 512],
                       k == 0, k == 7, [wb, bH[k]], bPS[b])
                blkid = (t0 // 128) + tb
                cp("act", VC[0:nt_, blkid, :], PS[b][0:nt_, :], [bPS[b]], [bVC[blkid]])
                q = tb % 2
                cp("dve", VST[0:nt_, q, :], PS[b][0:nt_, :], [bPS[b]], [bVST[q]])
                dma("sp", vout[t0 + tb * 128:t0 + tb * 128 + nt_, :], VST[0:nt_, q, :], "vo", [bVST[q]], [])
            fm_block("k")
            fm_block("q")
            fm_block("z")

        def conv(T):
            for c in range(8):
                b = 4 + c % 2
                for w in range(4):
                    mm(PS[b][:, 0:T], DG[:, c * 4 + w, :], XBC[:, c, w:w + T], w == 0, w == 3,
                       [B["DG"], bXBC[c]], bPS[b])
                if c < 4:
                    dst, db = XSb[:, c, 0:T], bXS[c]
                elif c < 6:
                    dst, db = BTb[:, c - 4, 0:T], bBT[c - 4]
                else:
                    dst, db = CTb[:, c - 6, 0:T], bCT[c - 6]
                silu_from_psum(T, PS[b][:, 0:T], bPS[b], dst, db, c % 2, hb=HCB[:, c:c + 1])

        GTb = CONB[:, C_GT:C_GT + 64]
        I64b = CONB[:, C_ID:C_ID + 64]
        NEGb = CONB[:, C_NEG:C_NEG + 512]

        def pb(name, q):
            return B[name if q == 0 else name + "1"]

        def ssd_1(ch):
            q = ch % 2
            cs = slice(ch * 64, (ch + 1) * 64)
            dA = DA[:, ch, :]
            tt("dve", RTH[:].rearrange("p (h l) -> p h l", h=8), TRI.unsqueeze(1).to_broadcast([64, 8, 64]),
               DAH[:, ch, :].unsqueeze(2).to_broadcast([64, 8, 64]), ALU.mult, [B["CON"], B["DAH"]], [B["RTH"]])
            tt("dve", RTL[:].rearrange("p (h l) -> p h l", h=8), TRI.unsqueeze(1).to_broadcast([64, 8, 64]),
               DAL[:, ch, :].unsqueeze(2).to_broadcast([64, 8, 64]), ALU.mult, [B["CON"], B["DAL"]], [B["RTL"]])
            P.group_begin("pe")
            mm(PS[0][0:64, :], GTb, RTH[:], True, False, [B["CONB"], B["RTH"]], bPS[0])
            mm(PS[0][0:64, :], GTb, RTL[:], False, False, [B["CONB"], B["RTL"]], bPS[0])
            mm(PS[0][0:64, :], I64b, NEGb, False, True, [B["CONB"]], bPS[0])
            mm(PS[1][:, :], ONEB[0:64, :], RTH[:], True, False, [B["CONST"], B["RTH"]], bPS[1])
            mm(PS[1][:, :], ONEB[0:64, :], RTL[:], False, True, [B["CONST"], B["RTL"]], bPS[1])
            mm(PS[2][0:64, 0:8], GT, dA, True, True, [B["CON"], B["DA"]], bPS[2])
            mm(PS[2][:, 8:16], ONEF[:], dA, True, True, [B["CONST"], B["DA"]], bPS[2])
            for g in range(2):
                mm(PS[2][0:64, 64 + g * 64:128 + g * 64], BTb[:, g, cs], CTb[:, g, cs], True, True,
                   [bBT[g], bCT[g]], bPS[2])
            P.group_end()
            act(LM[:, q, :], PS[0][0:64, :], AF.Exp, [bPS[0]], [pb("LM", q)])
            act(EA[:, q, :], PS[1][:, :], AF.Exp, [bPS[1]], [pb("EA", q)])
            act(DD[:, q, :], PS[2][0:64, 0:8], AF.Exp, [bPS[2]], [pb("DD", q)])
            act(CD[:, ch % 3, :], PS[2][:, 8:16], AF.Exp, [bPS[2]], [B["CD" + "abc"[ch % 3]]])
            cp("act", CBs[:, q, :], PS[2][0:64, 64:192], [bPS[2]], [pb("CBs", q)])

        def ssd_2a(ch):
            q = ch % 2
            cs = slice(ch * 64, (ch + 1) * 64)
            ps3 = PS[3][:].bitcast(BF16)
            P.group_begin("pe")
            for c in range(4):
                tr(ps3[0:64, c * 128:(c + 1) * 128], XSb[:, c, cs], IDB[:], [bXS[c], B["CONST"]], bPS[3])
            for g in range(2):
                tr(ps3[0:64, 512 + g * 128:512 + (g + 1) * 128], BTb[:, g, cs], IDB[:], [bBT[g], B["CONST"]], bPS[3])
            P.group_end()
            tt("dve", DD[:, q, :], DD[:, q, :], DT[:, ch, :], ALU.mult, [pb("DD", q), B["DT"]], [pb("DD", q)])
            tt("dve", XD[:, q, :].rearrange("p (h l) -> p h l", h=8), ps3[0:64, 0:512].rearrange("p (h l) -> p h l", h=8),
               DD[:, q, :].unsqueeze(2).to_broadcast([64, 8, 64]), ALU.mult, [bPS[3], pb("DD", q)], [pb("XD", q)])
            cp("act", XTk[:, q, :], ps3[0:64, 0:512], [bPS[3]], [pb("XTk", q)])
            cp("act", BTK[:, q, :], ps3[0:64, 512:768], [bPS[3]], [pb("BTK", q)])

        def ssd_2b(ch):
            q = ch % 2
            cs = slice(ch * 64, (ch + 1) * 64)
            tt("dve", LM[:, q, :].rearrange("p (g r l) -> p g r l", g=2, r=4),
               LM[:, q, :].rearrange("p (g r l) -> p g r l", g=2, r=4),
               CBs[:, q, :].rearrange("p (g l) -> p g l", g=2).unsqueeze(2).to_broadcast([64, 2, 4, 64]), ALU.mult,
               [pb("LM", q), pb("CBs", q)], [pb("LM", q)])
            tt("dve", WT[:, q, :].rearrange("p (h l) -> p h l", h=8), LM[:, q, :].rearrange("p (h l) -> p h l", h=8),
               DT[:, ch, :].unsqueeze(2).to_broadcast([64, 8, 64]), ALU.mult, [pb("LM", q), B["DT"]], [pb("WT", q)])
            tt("dve", CP[:, q, :].rearrange("p (g r l) -> p g r l", g=2, r=4),
               EA[:, q, :].rearrange("p (g r l) -> p g r l", g=2, r=4),
               CTb[:, :, cs].unsqueeze(2).to_broadcast([128, 2, 4, 64]), ALU.mult, [pb("EA", q), bCT[0], bCT[1]], [pb("CP", q)])

        def ssd_3(ch):
            q = ch % 2
            cs = slice(ch * 64, (ch + 1) * 64)
            P.group_begin("pe")
            for g in range(2):
                mm(PS[5][:, g * 256:(g + 1) * 256], BTK[:, q, g * 128:(g + 1) * 128], XD[:, q, g * 256:(g + 1) * 256],
                   True, True, [pb("BTK", q), pb("XD", q)], bPS[5])
            P.group_end()
            P.group_begin("pe")
            for h in range(8):
                po = (h % 2) * 64
                out = PS[4][po:po + 64, (h // 2) * 64:(h // 2) * 64 + 64]
                mm(out, XTk[:, q, h * 64:(h + 1) * 64], WT[:, q, h * 64:(h + 1) * 64], True, False,
                   [pb("XTk", q), pb("WT", q)], bPS[4])
                mm(out, SBb[:, h * 64:(h + 1) * 64], CP[:, q, h * 64:(h + 1) * 64], False, True,
                   [B["SBb"], pb("CP", q)], bPS[4])
            P.group_end()
            tt("dve", S[:].rearrange("p (h l) -> p h l", h=8), S[:].rearrange("p (h l) -> p h l", h=8),
               CD[:, ch % 3, :].unsqueeze(2).to_broadcast([128, 8, 64]), ALU.mult, [B["S"], B["CD" + "abc"[ch % 3]]], [B["S"]])
            tt("dve", S[:], S[:], PS[5][:, :], ALU.add, [B["S"], bPS[5]], [B["S"]])
            cp("act", SBb[:], S[:], [B["S"]], [B["SBb"]])
            cp("act", YS[:, :, cs], PS[4][:, 0:256].rearrange("p (c l) -> p c l", c=4), [bPS[4]], bYS)

        def ssd_post(T):
            for c in range(4):
                stt(YS[:, c, 0:T], XSb[:, c, 0:T], VEC[:, V_DS + c:V_DS + c + 1], YS[:, c, 0:T], ALU.mult, ALU.add,
                    [bXS[c], B["VEC"], bYS[c]], [bYS[c]])
                tt("dve", YS[:, c, 0:T], YS[:, c, 0:T], ZS[:, c, 0:T], ALU.mult, [bYS[c], bZS[c]], [bYS[c]])
                k = c % 2
                act(EB[:, k, 0:T], YS[:, c, 0:T], AF.Square, [bYS[c]], [bE[k]])
                mm(PS[7][:, 0:T], ONEB[:], EB[:, k, 0:T], c == 0, c == 3, [B["CONST"], bE[k]], bPS[7])
            act(RS[:, 0:T], PS[7][:, 0:T], AF.Ln, [bPS[7], B["CONST"]], [B["RS"]], bias=MHALF[:, 0:1], scale=1.0 / 512)
            act(RS[:, 0:T], RS[:, 0:T], AF.Exp, [B["RS"]], [B["RS"]], scale=-0.5)
            for c in range(4):
                stt(H[:, c, 0:T], YS[:, c, 0:T], VEC[:, V_SN + c:V_SN + c + 1], RS[:, 0:T], ALU.mult, ALU.mult,
                    [bYS[c], B["VEC"], B["RS"]], [bH[c]])

        def attention(T, kblocks):
            nb_ = len(kblocks)
            pending = [None]
            for h in range(4):
                def emit_S(bi, h=h):
                    kc0, nk, vblk, pbase, q0, ktile = kblocks[bi]
                    P.group_begin("pe")
                    for j in range(2):
                        sb_ = (bi % 2) * 2 + j
                        mm(PS[sb_][pbase:pbase + nk, q0:T], KT[j * 64:(j + 1) * 64, h, kc0:kc0 + nk],
                           QT[j * 64:(j + 1) * 64, h, q0:T], True, True, [bKT[h][ktile], bQT[h]], bPS[sb_])
                    P.group_end()
                    s0 = (bi % 2) * 2
                    act(EB[pbase:pbase + nk, s0:s0 + 2, q0:T], PSALL[pbase:pbase + nk, s0:s0 + 2, q0:T], AF.Exp,
                        [bPS[s0], bPS[s0 + 1]], [bE[s0], bE[s0 + 1]], scale=0.125)

                def emit_AV(bi, h=h):
                    kc0, nk, vblk, pbase, q0, ktile = kblocks[bi]
                    first, last = bi == 0, bi == nb_ - 1
                    P.group_begin("pe")
                    for j in range(2):
                        e_i = (bi % 2) * 2 + j
                        mm(PS[4 + j][:, q0:T], VC[pbase:pbase + nk, vblk, h * 128:(h + 1) * 128],
                           EB[pbase:pbase + nk, e_i, q0:T], first, last, [bVC[vblk], bE[e_i]], bPS[4 + j])
                    for j in range(2):
                        e_i = (bi % 2) * 2 + j
                        mm(PS[6 + j][:, q0:T], ONEB[pbase:pbase + nk, :], EB[pbase:pbase + nk, e_i, q0:T],
                           first, last, [B["CONST"], bE[e_i]], bPS[6 + j])
                    P.group_end()

                for i in range(nb_ + 1):
                    if i < nb_:
                        emit_S(i)
                    if i >= 1:
                        emit_AV(i - 1)
                    if i == 1 and pending[0] is not None:
                        pending[0]()
                        pending[0] = None
                for j in range(2):
                    act(OAv[j][:, 0:T], PS[6 + j][:, 0:T], AF.Ln, [bPS[6 + j]], [bOA[j]])
                for j in range(2):
                    act(OAv[j][:, 0:T], OAv[j][:, 0:T], AF.Exp, [bOA[j]], [bOA[j]], scale=-1.0)
                    tt("dve", OAv[j][:, 0:T], PS[4 + j][:, 0:T], OAv[j][:, 0:T], ALU.mult, [bPS[4 + j], bOA[j]], [bOA[j]])
                stt(OAv[2][:, 0:T], OAv[1][:, 0:T], NLAM[:, 0:1], OAv[0][:, 0:T], ALU.mult, ALU.add,
                    [bOA[0], bOA[1], B["NLAM"]], [bOA[2]])

                def part2(h=h):
                    act(SQB[:, 0:T], OAv[2][:, 0:T], AF.Square, [bOA[2]], [B["SQB"]])
                    mm(PS[0][:, 0:T], ONEB[:], SQB[:, 0:T], True, True, [B["CONST"], B["SQB"]], bPS[0])
                    act(RS[:, 0:T], PS[0][:, 0:T], AF.Ln, [bPS[0], B["CONST"]], [B["RS"]], bias=MHALF[:, 0:1], scale=1.0 / 128)
                    act(RS[:, 0:T], RS[:, 0:T], AF.Exp, [B["RS"]], [B["RS"]], scale=-0.5)
                    stt(H[:, 4 + h, 0:T], OAv[2][:, 0:T], 1.0 - LAMBDA_INIT, RS[:, 0:T], ALU.mult, ALU.mult,
                        [bOA[2], B["RS"]], [bH[4 + h]])
                pending[0] = part2
            pending[0]()

        def out_proj(T, s):
            for blk in range(2):
                wv, wb, _ = w_next()
                for cc in range(4):
                    m = blk * 4 + cc
                    b = m % 2
                    for k in range(8):
                        mm(PS[b][:, 0:T], wv[:, cc * 1024 + k * 128:cc * 1024 + (k + 1) * 128], H[:, k, 0:T],
                           k == 0, k == 7, [wb, bH[k]], bPS[b])
                    stt(X[:, m, 0:T], PS[b][:, 0:T], PV[:, s, 4, m:m + 1], X[:, m, 0:T], ALU.mult, ALU.add,
                        [bPS[b], B["PV"], bX[m]], [bX[m]])

        def final_norm(T, yout, t0):
            for c in range(8):
                k = c % 2
                act(EB[:, k, 0:T], X[:, c, 0:T], AF.Square, [bX[c]], [bE[k]])
                mm(PS[7][:, 0:T], ONEB[:], EB[:, k, 0:T], c == 0, c == 7, [B["CONST"], bE[k]], bPS[7])
            act(RS[:, 0:T], PS[7][:, 0:T], AF.Ln, [bPS[7], B["CONST"]], [B["RS"]], bias=MHALF[:, 0:1], scale=1.0 / D)
            act(RS[:, 0:T], RS[:, 0:T], AF.Exp, [B["RS"]], [B["RS"]], scale=-0.5)
            P.guard(bYO, bA + mixer_bufs)
            for c in range(8):
                stt(YO[:, c, 0:T], X[:, c, 0:T], VEC[:, V_FN + c:V_FN + c + 1], RS[:, 0:T], ALU.mult, ALU.mult,
                    [bX[c], B["VEC"], B["RS"]], [bYO[c]])
            dma("sp", yout.rearrange("(c p) t -> p c t", p=128)[:, :, t0:t0 + T], YO[:, :, 0:T], "yo", bYO, [])
            P.guard(bA + mixer_bufs, bYO)

        def tile(T, s, xin, t0, tile_idx, is_last, kblocks, yout, kout, vout):
            xin_v = xin.rearrange("(c p) t -> p c t", p=128)
            for i4 in range(4):
                dma("sp", X[:, 2 * i4:2 * i4 + 2, 0:T], xin_v[:, 2 * i4:2 * i4 + 2, t0:t0 + T], "x%d" % i4, [],
                    bX[2 * i4:2 * i4 + 2])
            rms_to_H(T, s, 0)
            P.guard(bA, mixer_bufs)
            ffn(T, s, 0)
            rms_to_H(T, s, 1)
            P.guard(mixer_bufs, bA)
            in_proj(T, s, t0, tile_idx, is_last, kout, vout)
            nch_ = T // 64
            for i in range(nch_ + 2):
                if 1 <= i <= nch_:
                    ssd_2a(i - 1)
                if i >= 2:
                    ssd_3(i - 2)
                if i < nch_:
                    ssd_1(i)
                if 1 <= i <= nch_:
                    ssd_2b(i - 1)
            ssd_post(T)
            attention(T, kblocks)
            out_proj(T, s)
            rms_to_H(T, s, 2)
            P.guard(bA, mixer_bufs)
            ffn(T, s, 2)
            final_norm(T, yout, t0)

        mset("dve", S[:], 0.0, [B["S"]])
        mset("dve", SBb[:], 0.0, [B["SBb"]])
        mset("dve", CVP[:], 0.0, [B["CVP"]])
        for t in range(NT):
            kbl = [(128 * kb, 128, kb, 0, 0, kb // 4) for kb in range(4 * t)]
            kbl += [(TT * t + 64 * c, 64, 4 * t + c // 2, (c % 2) * 64, 64 * c, t) for c in range(8)]
            tile(TT, 0, xT, t * TT, t, t == NT - 1, kbl, yT, kTo, vo)
        dma("sp", ssm_o, S[:], "misc", [B["S"]], [])
        dma("sp", conv_o, CVO[:], "misc", [B["CVO"]], [])

        if with_sample:
            for h in range(4):
                dma("pool", KT[:, h, 2048:4096], ckT[h * 128:(h + 1) * 128, :], "cache", [], bKT[h][4:8])
            for i in range(4):
                dma("pool", VC[:, 16 + 4 * i:20 + 4 * i, :],
                    cv[i * 512:(i + 1) * 512, :].rearrange("(b p) e -> p b e", p=128), "cache", [],
                    bVC[16 + 4 * i:20 + 4 * i])
            dma("sp", S[:], sssm, "misc", [], [B["S"]])
            cp("act", SBb[:], S[:], [B["S"]], [B["SBb"]])
            dma("sp", CVI[:], sconv, "misc", [], [B["CVI"]])
            cp("dve", CVP[:], CVI[:], [B["CVI"]], [B["CVP"]])
            kbl = [(2048 + 128 * i, 128, 16 + i, 0, 0, 4 + i // 4) for i in range(16)] + [(0, 64, 0, 0, 0, 0)]
            tile(SS, 1, xsT, 0, 0, True, kbl, ysT, ksTo, vso)
            dma("sp", ssms_o, S[:], "misc", [B["S"]], [])
            dma("sp", convs_o, CVO[:], "misc", [B["CVO"]], [])

        P.emit(nc, block, sems, lane_sems)
    return nc


def _chunk128(v):
    return np.ascontiguousarray(v.reshape(-1, 128).T)


def _lhs_chunks(w):
    K, M = w.shape
    out = []
    for m in range(M // 128):
        blk = w[:, m * 128:(m + 1) * 128].reshape(K // 128, 128, 128).transpose(1, 0, 2).reshape(128, -1)
        out.append(blk)
    return out


def _build_wall(ffn1_wgu, ffn1_wd, w_in, w_out, ffn2_wgu, ffn2_wd):
    def ffn_blocks(wgu, wd):
        g = _lhs_chunks(wgu[:, :DFF])
        u = _lhs_chunks(wgu[:, DFF:])
        blks = []
        for jb in range(11):
            blks.append(np.concatenate([g[2 * jb], u[2 * jb], g[2 * jb + 1], u[2 * jb + 1]], axis=1))
        blks += _lhs_chunks(wd)
        return blks
    z, xbc, dt, q, k, v = (w_in[:, 0:512], w_in[:, 512:1536], w_in[:, 1536:1544], w_in[:, 1544:2056],
                           w_in[:, 2056:2568], w_in[:, 2568:3080])
    fm = _lhs_chunks(xbc) + _lhs_chunks(k) + _lhs_chunks(q) + _lhs_chunks(z)
    fmb = [np.concatenate(fm[4 * i:4 * i + 4], axis=1) for i in range(5)]
    vb = v.reshape(8, 128, 512).transpose(1, 0, 2).reshape(128, -1)
    dtb = dt.reshape(8, 128, 8).transpose(1, 0, 2).reshape(128, -1)
    inb = [fmb[0], fmb[1], np.concatenate([vb, dtb], axis=1), fmb[2], fmb[3], fmb[4]]
    om = _lhs_chunks(w_out)
    outb = [np.concatenate(om[0:4], axis=1), np.concatenate(om[4:8], axis=1)]
    allb = ffn_blocks(ffn1_wgu, ffn1_wd) + inb + outb + ffn_blocks(ffn2_wgu, ffn2_wd)
    assert [b.shape[1] for b in allb] == BLKS
    return np.ascontiguousarray(np.concatenate(allb, axis=1), dtype=np.float32)


def _consts():
    con = np.zeros((128, NCON), np.float32)
    con[:, C_ID:C_ID + 128] = np.eye(128, dtype=np.float32)
    t = np.arange(64)
    con[0:64, C_GT:C_GT + 64] = (t[:, None] > t[None, :]).astype(np.float32)
    con[0:64, C_TRI:C_TRI + 64] = (t[:, None] <= t[None, :]).astype(np.float32)
    neg = np.where(t[:, None] > t[None, :], -30000.0, 0.0).astype(np.float32)
    con[0:64, C_NEG:C_NEG + 64] = neg
    return con


_NC_CACHE = {}


def _prep(inputs, NT):
    f = lambda a: np.asarray(a, dtype=np.float32)
    wall = _build_wall(f(inputs["ffn1_wgu"])[0], f(inputs["ffn1_wd"])[0], f(inputs["w_in"])[0],
                       f(inputs["w_out"])[0], f(inputs["ffn2_wgu"])[0], f(inputs["ffn2_wd"])[0])
    con = _consts()
    vec = np.zeros((128, NVEC), np.float32)
    vec[:, V_N1:V_N1 + 8] = _chunk128(f(inputs["norm1"])[0])
    vec[:, V_N2:V_N2 + 8] = _chunk128(f(inputs["norm2"])[0])
    vec[:, V_N3:V_N3 + 8] = _chunk128(f(inputs["norm3"])[0])
    vec[:, V_FN:V_FN + 8] = _chunk128(f(inputs["final_norm"]))
    vec[:, V_CB:V_CB + 8] = _chunk128(f(inputs["conv_b"])[0])
    cw = f(inputs["conv_w"])[0]
    for w in range(4):
        vec[:, V_CW + w * 8:V_CW + w * 8 + 8] = _chunk128(cw[w])
    vec[:, V_SN:V_SN + 4] = _chunk128(f(inputs["ssd_norm"])[0])
    vec[:, V_DS:V_DS + 4] = _chunk128(np.repeat(f(inputs["d_skip"])[0], 64))
    vec[:, V_DTB:V_DTB + 8] = f(inputs["dt_bias"])[0][None, :]
    vec[:, V_ALOG:V_ALOG + 8] = f(inputs["a_log"])[0][None, :]
    lam = np.concatenate([f(inputs[n])[0] for n in ("lam_q1", "lam_k1", "lam_q2", "lam_k2")])
    vec[:, V_LAM:V_LAM + 256] = lam[None, :]
    vec[:, V_BADA:V_BADA + 72] = _chunk128(f(inputs["b_ada"])[0])
    wada = np.ascontiguousarray(f(inputs["w_ada"])[0])
    xp, xs = f(inputs["x_prompt"]), f(inputs["x_sample"])
    cp_, cs_ = f(inputs["c_prompt"]), f(inputs["c_sample"])
    ck, cvv = f(inputs["cache_k"])[0], f(inputs["cache_v"])[0]
    st, sc = f(inputs["state_ssm"])[0], f(inputs["state_conv"])[0]
    maps = []
    for b in range(8):
        cT = np.stack([_chunk128(cp_[b]), _chunk128(cs_[b])], axis=2).reshape(128, 16)
        maps.append({
            "xT": np.ascontiguousarray(xp[b, :NT * TT].T),
            "xsT": np.ascontiguousarray(xs[b].T),
            "cT": np.ascontiguousarray(cT),
            "wada": wada, "vec": vec, "con": con, "wall": wall,
            "ckT": np.ascontiguousarray(ck[b].reshape(PAST, 512).T),
            "cv": np.ascontiguousarray(cvv[b].reshape(PAST, 512)),
            "sssm": np.ascontiguousarray(st[b].reshape(512, 128).T),
            "sconv": np.ascontiguousarray(sc[b].reshape(3, 8, 128).transpose(2, 1, 0).reshape(128, 24)),
        })
    return maps


def _gather(res, NT):
    L = NT * TT
    g = lambda n: [np.asarray(r[n], dtype=np.float32) for r in res]
    y = np.stack([a.T for a in g("yT")])
    ys = np.stack([a.T for a in g("ysT")])
    nk = np.stack([a.T.reshape(L, 4, 2, 64) for a in g("kTo")])[None]
    nv = np.stack([a.reshape(L, 4, 128) for a in g("vo")])[None]
    ssm = np.stack([a.T.reshape(8, 64, 128) for a in g("ssm_o")])[None]
    conv = np.stack([a.reshape(128, 8, 3).transpose(2, 1, 0).reshape(3, 1024) for a in g("conv_o")])[None]
    nks = np.stack([a.T.reshape(SS, 4, 2, 64) for a in g("ksTo")])[None]
    nvs = np.stack([a.reshape(SS, 4, 128) for a in g("vso")])[None]
    ssms = np.stack([a.T.reshape(8, 64, 128) for a in g("ssms_o")])[None]
    convs = np.stack([a.reshape(128, 8, 3).transpose(2, 1, 0).reshape(3, 1024) for a in g("convs_o")])[None]
    return tuple(np.ascontiguousarray(a, dtype=np.float32) for a in
                 (y, ys, nk, nv, ssm, conv, nks, nvs, ssms, convs))


def run(inputs, NT=8, cores=8):
    maps = _prep(inputs, NT)[:cores]
    nc = build(NT)
    res = run_bass_kernel_spmd(nc, maps, core_ids=list(range(cores)))
    return _gather(res.results, NT)


def kernel(**inputs):
    return run(inputs, 8, 8)
```

```python
import numpy as np
import concourse.bass as bass
import concourse.mybir as mybir
from concourse.bass_utils import run_bass_kernel_spmd

F32 = mybir.dt.float32
BF16 = mybir.dt.bfloat16
AF = mybir.ActivationFunctionType
ALU = mybir.AluOpType

D = 1024
DFF = 2816
NPAIR = 22
TT = 512
SEQ = 4096
SS = 64
PAST = 2048
EPS = 1e-6
LAMBDA_INIT = 0.8 - 0.6 * 1.0
NSLOT = 3
SLOT = 4160

BLK_FFN = [4096] * 11 + [2816] * 8
BLK_IN = [4096, 4096, 4160, 4096, 4096, 4096]
BLK_OUT = [4096] * 2
BLKS = BLK_FFN + BLK_IN + BLK_OUT + BLK_FFN
WTOT = sum(BLKS)

V_N1, V_N2, V_N3, V_FN, V_CB, V_CW, V_SN, V_DS, V_DTB, V_ALOG, V_LAM, V_BADA, NVEC = (
    0, 8, 16, 24, 32, 40, 72, 76, 80, 88, 96, 352, 424)
C_ID, C_GT, C_TRI, C_NEG, NCON = 0, 128, 192, 256, 320
NCONB = 768


class Buf:
    __slots__ = ("name", "w", "r", "excl")

    def __init__(self, name, excl=False):
        self.name = name
        self.w = {}
        self.r = {}
        self.excl = excl


def _merge(d, tok):
    k, v = tok
    if d.get(k, 0) < v:
        d[k] = v


class Prog:
    ENGS = ("pe", "act", "dve", "pool", "sp")

    def __init__(self):
        self.ops = {e: [] for e in self.ENGS}
        self.lanes = {}

    def _record(self, tok, reads, writes):
        deps = {}
        R = [b for b in reads if not b.excl]
        W = list(writes) + [b for b in reads if b.excl]
        for b in R:
            for t in b.w.items():
                _merge(deps, t)
        for b in W:
            for t in b.w.items():
                _merge(deps, t)
            for t in b.r.items():
                _merge(deps, t)
        for b in R:
            _merge(b.r, tok)
        for b in W:
            b.w = {tok[0]: tok[1]}
            b.r = {}
        return deps

    def op(self, eng, fn, reads=(), writes=()):
        idx = len(self.ops[eng]) + 1
        tok = (eng, idx)
        deps = self._record(tok, reads, writes)
        self.ops[eng].append(dict(fn=fn, deps=deps, lane=None))

    def dma(self, eng, fn, lane, reads=(), writes=()):
        n = self.lanes.get(lane, 0) + 1
        self.lanes[lane] = n
        tok = ("L:" + lane, n)
        deps = self._record(tok, reads, writes)
        if n > 1:
            _merge(deps, ("L:" + lane, n - 1))
        self.ops[eng].append(dict(fn=fn, deps=deps, lane=lane))

    def group_begin(self, eng):
        self._g = (eng, len(self.ops[eng]), {e: len(self.ops[e]) for e in self.ENGS}, dict(self.lanes))

    def group_end(self):
        eng, start, counts, lanes = self._g
        ops = self.ops[eng][start:]
        if len(ops) > 1:
            merged = {}
            for o in ops:
                for k, v in o["deps"].items():
                    if k == eng:
                        continue
                    if k.startswith("L:"):
                        assert v <= lanes.get(k[2:], 0), "group dep on later dma"
                    else:
                        assert v <= counts[k], "group dep on later op"
                    _merge(merged, (k, v))
            for k, v in ops[0]["deps"].items():
                if k == eng:
                    merged[k] = v
            ops[0]["deps"] = merged

    def guard(self, new_bufs, old_bufs):
        for nb in new_bufs:
            for ob in old_bufs:
                for t in ob.w.items():
                    _merge(nb.r, t)
                for t in ob.r.items():
                    _merge(nb.r, t)

    def emit(self, nc, block, sems, lane_sems):
        marked = {e: set() for e in self.ENGS}
        for e in self.ENGS:
            for o in self.ops[e]:
                for k, v in o["deps"].items():
                    if k.startswith("L:"):
                        continue
                    if k == e and e == "pe":
                        continue
                    marked[k].add(v)
        cnt = {}
        for e in self.ENGS:
            c = 0
            m = {}
            last = 0
            ms = marked[e]
            for i in range(1, len(self.ops[e]) + 1):
                if i in ms:
                    c += 1
                m[i] = c
            cnt[e] = m
        engobj = dict(pe=nc.tensor, act=nc.scalar, dve=nc.vector, pool=nc.gpsimd, sp=nc.sync)
        binder = dict(pe=block.tensor, act=block.scalar, dve=block.vector, pool=block.gpsimd,
                      sp=block.sync)
        final_lanes = dict(self.lanes)

        def make(e):
            def body(eng):
                waited = {}
                ms = marked[e]
                for i, o in enumerate(self.ops[e], start=1):
                    for k, v in o["deps"].items():
                        if k.startswith("L:"):
                            sem = lane_sems[k[2:]]
                            val = 16 * v
                        else:
                            if k == e and e == "pe":
                                continue
                            sem = sems[k]
                            val = cnt[k][v]
                        if waited.get(k, 0) >= val:
                            continue
                        waited[k] = val
                        eng.wait_ge(sem, val)
                    ins = o["fn"](eng)
                    if o["lane"] is not None:
                        ins.then_inc(lane_sems[o["lane"]], 16)
                    elif i in ms:
                        ins.then_inc(sems[e], 1)
                if e == "sp":
                    for ln, n in final_lanes.items():
                        eng.wait_ge(lane_sems[ln], 16 * n)
            return body

        for e in self.ENGS:
            binder[e](make(e))


def build(NT=8, with_sample=True):
    nc = bass.Bass("TRN2", target_bir_lowering=False)
    LP = NT * TT
    dr = lambda name, shape, kind, dt=F32: nc.dram_tensor(name, shape, dt, kind=kind).ap()
    xT = dr("xT", [D, LP], "ExternalInput")
    xsT = dr("xsT", [D, SS], "ExternalInput")
    cT = dr("cT", [128, 16], "ExternalInput")
    wada = dr("wada", [D, 9 * D], "ExternalInput")
    vec_d = dr("vec", [128, NVEC], "ExternalInput")
    con_d = dr("con", [128, NCON], "ExternalInput")
    wall = dr("wall", [128, WTOT], "ExternalInput")
    ckT = dr("ckT", [512, PAST], "ExternalInput")
    cv = dr("cv", [PAST, 512], "ExternalInput")
    sssm = dr("sssm", [128, 512], "ExternalInput")
    sconv = dr("sconv", [128, 24], "ExternalInput")
    scr = dr("scr", [128, WTOT], "Internal", BF16)
    yT = dr("yT", [D, LP], "ExternalOutput")
    ysT = dr("ysT", [D, SS], "ExternalOutput")
    kTo = dr("kTo", [512, LP], "ExternalOutput")
    vo = dr("vo", [LP, 512], "ExternalOutput")
    ksTo = dr("ksTo", [512, SS], "ExternalOutput")
    vso = dr("vso", [SS, 512], "ExternalOutput")
    ssm_o = dr("ssm_o", [128, 512], "ExternalOutput")
    ssms_o = dr("ssms_o", [128, 512], "ExternalOutput")
    conv_o = dr("conv_o", [128, 24], "ExternalOutput")
    convs_o = dr("convs_o", [128, 24], "ExternalOutput")

    P = Prog()
    from contextlib import ExitStack
    es = ExitStack()
    sb = lambda name, shape, dt=F32: es.enter_context(nc.sbuf_tensor(name, shape, dt))
    with es:
        X = sb("X", [128, 8, TT])
        H = sb("H", [128, 8, TT], BF16)
        U = sb("U", [128, 8448])
        RING = sb("RING", [128, NSLOT, SLOT], BF16)
        KT = sb("KT", [128, 4, SEQ], BF16)
        VC = sb("VC", [128, 32, 512], BF16)
        KST = sb("KST", [128, 2, TT])
        VST = sb("VST", [128, 2, 512])
        VEC = sb("VEC", [128, NVEC])
        CON = sb("CON", [128, NCON])
        IDB = sb("IDB", [128, 128], BF16)
        ONEB = sb("ONEB", [128, 128], BF16)
        ONEF = sb("ONEF", [64, 128])
        MHALF = sb("MHALF", [128, 1])
        DG = sb("DG", [128, 32, 128], BF16)
        CTs = sb("CTs", [128, 16])
        SC = sb("SC", [128, 16])
        SCb = sb("SCb", [128, 16], BF16)
        MOD = sb("MOD", [128, 144])
        PV = sb("PV", [128, 2, 6, 8])
        HCB = sb("HCB", [128, 8])
        AN = sb("AN", [128, 8])
        LS = sb("LS", [128, 4])
        NLAM = sb("NLAM", [128, 1])
        S = sb("S", [128, 512])
        SBb = sb("SBb", [128, 512], BF16)
        CVO = sb("CVO", [128, 24])
        CVI = sb("CVI", [128, 24])
        RS = sb("RS", [128, TT])
        LT = RS[:, 0:256]
        TM = sb("TM", [128, 2, TT])
        TM2 = sb("TM2", [128, 2, TT])
        EB = sb("EB", [128, 4, TT], BF16)
        CVP = sb("CVP", [128, 24], BF16)
        SQB = sb("SQB", [128, TT], BF16)
        DTR = sb("DTR", [64, 8, 8])
        DT = sb("DT", [64, 8, 8])
        DA = sb("DA", [64, 8, 8])
        RTH = sb("RTH", [64, 512], BF16)
        RTL = sb("RTL", [64, 512], BF16)
        CONB = sb("CONB", [64, NCONB], BF16)
        DAH = sb("DAH", [64, 8, 8], BF16)
        DAL = sb("DAL", [64, 8, 8], BF16)
        LM = sb("LM", [64, 2, 512])
        EA = sb("EA", [128, 2, 512])
        DD = sb("DD", [64, 2, 8])
        CD = sb("CD", [128, 3, 8])
        CBs = sb("CBs", [64, 2, 128])
        WT = sb("WT", [64, 2, 512], BF16)
        XTk = sb("XTk", [64, 2, 512], BF16)
        XD = sb("XD", [64, 2, 512], BF16)
        BTK = sb("BTK", [64, 2, 256], BF16)
        CP = sb("CP", [128, 2, 512], BF16)
        Ub = U[:].bitcast(BF16)
        A_v = Ub[:, 0:NPAIR * TT].rearrange("p (j t) -> p j t", j=NPAIR)
        o = 0
        ZS = Ub[:, o:o + 4 * TT].rearrange("p (c t) -> p c t", c=4); o += 4 * TT
        XBC = Ub[:, o:o + 8 * 520].rearrange("p (c t) -> p c t", c=8); o += 8 * 520
        XSb = Ub[:, o:o + 4 * TT].rearrange("p (c t) -> p c t", c=4); o += 4 * TT
        BTb = Ub[:, o:o + 2 * TT].rearrange("p (c t) -> p c t", c=2); o += 2 * TT
        CTb = Ub[:, o:o + 2 * TT].rearrange("p (c t) -> p c t", c=2); o += 2 * TT
        QT = Ub[:, o:o + 4 * TT].rearrange("p (c t) -> p c t", c=4); o += 4 * TT
        assert o % 2 == 0
        YS = U[:, o // 2:o // 2 + 4 * TT].rearrange("p (c t) -> p c t", c=4); o += 8 * TT
        assert o <= 16896
        YO = U[:, 0:8 * TT].rearrange("p (c t) -> p c t", c=8)
        WA0 = KT[:, 0:2, :].rearrange("p h t -> p (h t)").rearrange("p (k c) -> p k c", k=8)
        WA1 = KT[:, 2:4, :].rearrange("p h t -> p (h t)").rearrange("p (k c) -> p k c", k=8)
        PSALL = es.enter_context(nc.psum_tensor("psall", [128, 8, 512], F32))
        PS = [PSALL[:, i, :] for i in range(8)]
        bPS = [Buf(f"ps{i}", excl=True) for i in range(8)]
        sems = {e: es.enter_context(nc.semaphore("sem_" + e)) for e in ("pe", "act", "dve", "pool")}
        lane_names = (["c0", "c1", "wa0", "wa1", "cvt0", "cvt1", "cvt2", "cvt3", "x0", "x1", "x2", "x3", "scrw", "yo", "ko", "vo", "misc", "cache"]
                      + [f"r{i}" for i in range(NSLOT)])
        lane_sems = {l: es.enter_context(nc.semaphore("ln_" + l)) for l in lane_names}
        block = es.enter_context(nc.Block())

        bX = [Buf(f"X{c}") for c in range(8)]
        bH = [Buf(f"H{c}") for c in range(8)]
        bA = [Buf(f"A{j}") for j in range(NPAIR)]
        bRING = [Buf(f"ring{i}") for i in range(NSLOT)]
        bKT = [[Buf(f"KT{h}_{t}") for t in range(8)] for h in range(4)]
        bVC = [Buf(f"VC{b}") for b in range(32)]
        bKST = [Buf("KST0"), Buf("KST1")]
        bVST = [Buf("VST0"), Buf("VST1")]
        bZS = [Buf(f"ZS{c}") for c in range(4)]
        bXBC = [Buf(f"XBC{c}") for c in range(8)]
        bXS = [Buf(f"XS{c}") for c in range(4)]
        bBT = [Buf(f"BT{c}") for c in range(2)]
        bCT = [Buf(f"CT{c}") for c in range(2)]
        bQT = [Buf(f"QT{c}") for c in range(4)]
        bYS = [Buf(f"YS{c}") for c in range(4)]
        bYO = [Buf(f"YO{c}") for c in range(8)]
        mixer_bufs = bZS + bXBC + bXS + bBT + bCT + bQT + bYS
        bWA = [Buf("WA0"), Buf("WA1")]
        bSCR = [Buf(f"scr{i}") for i in range(len(BLKS))]
        B = {n: Buf(n) for n in (
            "VEC", "CON", "CONST", "CTs", "SC", "SCb", "MOD", "PV", "AN", "LT", "LS", "NLAM", "S", "SBb",
            "CVO", "CVI", "RS", "TM0", "TM1", "TM20", "TM21", "E0", "E1", "E2", "E3",
            "CVP", "SQB", "DTR", "DT", "DA", "RT", "LM", "EA", "DD", "CD", "CBs", "WT", "XTk", "XD",
            "BTK", "CP", "DG", "RTH", "RTL", "CONB", "DAH", "DAL", "WT1", "XTk1", "XD1", "BTK1", "CP1", "CD1", "LM1", "EA1", "DD1", "CBs1", "CDa", "CDb", "CDc")}
        bTM = [B["TM0"], B["TM1"]]
        bTM2 = [B["TM20"], B["TM21"]]
        bE = [B["E0"], B["E1"], B["E2"], B["E3"]]
        bOA = [bTM[0], bTM[1], bTM2[0]]
        OAv = [TM[:, 0, :], TM[:, 1, :], TM2[:, 0, :]]

        def mm(out, lhsT, rhs, start, stop, reads, bank):
            P.op("pe", lambda e, o=out, l=lhsT, r=rhs, s=start, t=stop:
                 e.matmul(o, lhsT=l, rhs=r, start=s, stop=t, skip_group_check=True),
                 reads=reads, writes=[bank])

        def tr(out, in_, ident, reads, bank):
            P.op("pe", lambda e, o=out, i=in_, d=ident: e.transpose(o, i, d),
                 reads=reads, writes=[bank])

        def act(out, in_, func, reads, writes, bias=None, scale=None):
            kw = {}
            if bias is not None:
                kw["bias"] = bias
            if scale is not None:
                kw["scale"] = scale
            P.op("act", lambda e, o=out, i=in_, f=func, k=kw: e.activation(o, i, f, **k),
                 reads=reads, writes=writes)

        def tt(eng, out, in0, in1, op, reads, writes):
            P.op(eng, lambda e, o=out, a=in0, b=in1, p=op: e.tensor_tensor(out=o, in0=a, in1=b, op=p),
                 reads=reads, writes=writes)

        def ts(eng, out, in0, s1, op0, reads, writes, s2=None, op1=None):
            if op1 is None:
                P.op(eng, lambda e, o=out, a=in0, x=s1, p=op0:
                     e.tensor_scalar(out=o, in0=a, scalar1=x, scalar2=None, op0=p),
                     reads=reads, writes=writes)
            else:
                P.op(eng, lambda e, o=out, a=in0, x=s1, y=s2, p=op0, q=op1:
                     e.tensor_scalar(out=o, in0=a, scalar1=x, scalar2=y, op0=p, op1=q),
                     reads=reads, writes=writes)

        def stt(out, in0, scalar, in1, op0, op1, reads, writes):
            P.op("dve", lambda e, o=out, a=in0, s=scalar, b=in1, p=op0, q=op1:
                 e.scalar_tensor_tensor(out=o, in0=a, scalar=s, in1=b, op0=p, op1=q),
                 reads=reads, writes=writes)

        def cp(eng, out, in_, reads, writes):
            if eng == "act":
                P.op("act", lambda e, o=out, i=in_: e.copy(o, i), reads=reads, writes=writes)
            else:
                P.op(eng, lambda e, o=out, i=in_: e.tensor_copy(out=o, in_=i), reads=reads, writes=writes)

        def dma(eng, out, in_, lane, reads, writes):
            P.dma(eng, lambda e, o=out, i=in_: e.dma_start(out=o, in_=i), lane, reads=reads, writes=writes)

        def mset(eng, ap, val, writes):
            P.op(eng, lambda e, a=ap, v=val: e.memset(a, v), writes=writes)

        dma("sp", VEC[:], vec_d, "c0", [], [B["VEC"]])
        dma("sp", CON[:], con_d, "c1", [], [B["CON"]])
        dma("sp", CTs[:], cT, "c0", [], [B["CTs"]])
        cst = B["CONST"]
        mset("pool", ONEB[:], 1.0, [cst])
        mset("pool", ONEF[:], 1.0, [cst])
        mset("pool", MHALF[:], EPS, [cst])
        cp("dve", IDB[:], CON[:, C_ID:C_ID + 128], [B["CON"]], [cst])
        cp("dve", CONB[:, 0:256], CON[0:64, 0:256], [B["CON"]], [B["CONB"]])
        cp("dve", CONB[:, 256:768].rearrange("p (h l) -> p h l", h=8),
           CON[0:64, C_NEG:C_NEG + 64].unsqueeze(1).to_broadcast([64, 8, 64]), [B["CON"]], [B["CONB"]])
        IDF = CON[:, C_ID:C_ID + 128]
        I64 = CON[0:64, C_ID:C_ID + 64]
        GT = CON[0:64, C_GT:C_GT + 64]
        TRI = CON[0:64, C_TRI:C_TRI + 64]
        for c in range(8):
            for w in range(4):
                col = V_CW + w * 8 + c
                ts("pool" if (c + w) % 2 else "dve", DG[:, c * 4 + w, :], IDF, VEC[:, col:col + 1], ALU.mult,
                   [B["CON"], B["VEC"]], [B["DG"]])
        ts("dve", HCB[:], VEC[:, V_CB:V_CB + 8], 0.5, ALU.mult, [B["VEC"]], [cst])
        act(AN[:], VEC[:, V_ALOG:V_ALOG + 8], AF.Exp, [B["VEC"]], [B["AN"]])
        ts("dve", AN[:], AN[:], -1.0, ALU.mult, [B["AN"]], [B["AN"]])
        tt("dve", LT[:, 0:64], VEC[:, V_LAM:V_LAM + 64], VEC[:, V_LAM + 64:V_LAM + 128], ALU.mult,
           [B["VEC"]], [B["LT"]])
        tt("dve", LT[:, 64:128], VEC[:, V_LAM + 128:V_LAM + 192], VEC[:, V_LAM + 192:V_LAM + 256], ALU.mult,
           [B["VEC"]], [B["LT"]])
        P.op("dve", lambda e: e.reduce_sum(out=LS[:, 0:2], in_=LT[:, 0:128].rearrange("p (a b) -> p a b", a=2),
                                           axis=mybir.AxisListType.X), reads=[B["LT"]], writes=[B["LS"]])
        P.guard([B["RS"]], [B["LT"]])
        act(LS[:, 2:4], LS[:, 0:2], AF.Exp, [B["LS"]], [B["LS"]])
        tt("dve", NLAM[:], LS[:, 3:4], LS[:, 2:3], ALU.subtract, [B["LS"]], [B["NLAM"]])
        ts("dve", NLAM[:], NLAM[:], -LAMBDA_INIT, ALU.add, [B["NLAM"]], [B["NLAM"]])
        act(SC[:], CTs[:], AF.Tanh, [B["CTs"]], [B["SC"]], scale=0.5)
        stt(SC[:], SC[:], 1.0, CTs[:], ALU.add, ALU.mult, [B["SC"], B["CTs"]], [B["SC"]])
        ts("dve", SC[:], SC[:], 0.5, ALU.mult, [B["SC"]], [B["SC"]])
        cp("dve", SCb[:], SC[:], [B["SC"]], [B["SCb"]])
        wada_v = wada.rearrange("(k p) n -> p k n", p=128)
        WAs = [WA0, WA1]
        for i in range(9):
            s = i % 2
            dma("pool", WAs[s], wada_v[:, :, i * D:(i + 1) * D], "wa%d" % s, [], [bWA[s]])
            for jj in range(8):
                j = i * 8 + jj
                for k in range(8):
                    mm(PS[7][:, 2 * j:2 * j + 2], WAs[s][:, k, jj * 128:(jj + 1) * 128],
                       SCb[:].rearrange("p (k s) -> p k s", s=2)[:, k, :], k == 0, k == 7,
                       [bWA[s], B["SCb"]], bPS[7])
        tt("dve", MOD[:].rearrange("p (j s) -> p j s", s=2), PS[7][:, 0:144].rearrange("p (j s) -> p j s", s=2),
           VEC[:, V_BADA:V_BADA + 72].unsqueeze(2).to_broadcast([128, 72, 2]), ALU.add,
           [bPS[7], B["VEC"]], [B["MOD"]])
        MODv = MOD[:].rearrange("p (j s) -> p j s", s=2)
        for s in range(2):
            for n in range(3):
                sc_j = 8 * (3 * n + 1)
                stt(PV[:, s, n, :], MODv[:, sc_j:sc_j + 8, s], 1.0, VEC[:, 8 * n:8 * n + 8], ALU.add, ALU.mult,
                    [B["MOD"], B["VEC"]], [B["PV"]])
                g_j = 8 * (3 * n + 2)
                ts("dve", PV[:, s, 3 + n, :], MODv[:, g_j:g_j + 8, s], 1.0 if n == 1 else 0.25, ALU.mult,
                   [B["MOD"]], [B["PV"]])

        def shv(s, n, c):
            j = 8 * 3 * n + c
            return MOD[:, 2 * j + s:2 * j + s + 1]

        P.guard([b for row in bKT for b in row] + bVC, bWA)

        offs = np.concatenate([[0], np.cumsum(BLKS)]).astype(int)

        class WS:
            issued = 0
            cur = 0
            total = 0
        nblk = len(BLKS)

        def w_issue():
            g = WS.issued
            i = g % nblk
            s = g % NSLOT
            n = BLKS[i]
            if g < nblk:
                dma("pool", RING[:, s, 0:n], wall[:, offs[i]:offs[i] + n], "cvt%d" % (g % 4), [], [bRING[s]])
                dma("sp", scr[:, offs[i]:offs[i] + n], RING[:, s, 0:n], "scrw", [bRING[s]], [bSCR[i]])
            else:
                dma("sp", RING[:, s, 0:n], scr[:, offs[i]:offs[i] + n], "r%d" % s, [bSCR[i]], [bRING[s]])
            WS.issued += 1

        def w_next():
            g = WS.cur
            while WS.issued < min(WS.total, g + NSLOT - 1 + 1):
                w_issue()
            WS.cur += 1
            s = g % NSLOT
            assert g < WS.issued
            return RING[:, s, :], bRING[s], BLKS[g % nblk]

        ntiles = NT + (1 if with_sample else 0)
        WS.total = ntiles * nblk

        def rms_to_H(T, s, n):
            for c in range(8):
                act(H[:, c, 0:T], X[:, c, 0:T], AF.Square, [bX[c]], [bH[c]])
                mm(PS[7][:, 0:T], ONEB[:], H[:, c, 0:T], c == 0, c == 7, [B["CONST"], bH[c]], bPS[7])
            act(RS[:, 0:T], PS[7][:, 0:T], AF.Ln, [bPS[7], B["CONST"]], [B["RS"]], bias=MHALF[:, 0:1], scale=1.0 / D)
            act(RS[:, 0:T], RS[:, 0:T], AF.Exp, [B["RS"]], [B["RS"]], scale=-0.5)
            for c in range(8):
                k = c % 2
                tt("dve", TM[:, k, 0:T], X[:, c, 0:T], RS[:, 0:T], ALU.mult, [bX[c], B["RS"]], [bTM[k]])
                act(H[:, c, 0:T], TM[:, k, 0:T], AF.Identity, [bTM[k], B["PV"], B["MOD"]], [bH[c]],
                    bias=shv(s, n, c), scale=PV[:, s, n, c:c + 1])

        def ffn(T, s, gate_n):
            for jb in range(11):
                wv, wb, _ = w_next()
                for jj in range(2):
                    j = 2 * jb + jj
                    pg, pu = (0, 1) if j % 2 == 0 else (2, 3)
                    base = jj * 2048
                    if j > 0:
                        P.group_begin("pe")
                    for k in range(8):
                        mm(PS[pg][:, 0:T], wv[:, base + k * 128:base + (k + 1) * 128], H[:, k, 0:T],
                           k == 0, k == 7, [wb, bH[k]], bPS[pg])
                    for k in range(8):
                        mm(PS[pu][:, 0:T], wv[:, base + 1024 + k * 128:base + 1024 + (k + 1) * 128], H[:, k, 0:T],
                           k == 0, k == 7, [wb, bH[k]], bPS[pu])
                    if j > 0:
                        P.group_end()
                    q = j % 2
                    act(TM[:, q, 0:T], PS[pg][:, 0:T], AF.Tanh, [bPS[pg]], [bTM[q]], scale=0.5)
                    stt(TM[:, q, 0:T], TM[:, q, 0:T], 1.0, PS[pg][:, 0:T], ALU.add, ALU.mult,
                        [bTM[q], bPS[pg]], [bTM[q]])
                    tt("dve", A_v[:, j, 0:T], TM[:, q, 0:T], PS[pu][:, 0:T], ALU.mult, [bTM[q], bPS[pu]], [bA[j]])
            for m in range(8):
                wv, wb, _ = w_next()
                pb = 4 + m % 2
                if m > 0:
                    P.group_begin("pe")
                for k in range(NPAIR):
                    mm(PS[pb][:, 0:T], wv[:, k * 128:(k + 1) * 128], A_v[:, k, 0:T], k == 0, k == NPAIR - 1,
                       [wb, bA[k]], bPS[pb])
                if m > 0:
                    P.group_end()
                stt(X[:, m, 0:T], PS[pb][:, 0:T], PV[:, s, 3 + gate_n, m:m + 1], X[:, m, 0:T], ALU.mult, ALU.add,
                    [bPS[pb], B["PV"], bX[m]], [bX[m]])

        def silu_from_psum(T, ps, bank, dst, dstbuf, q, hb=None):
            if hb is None:
                act(TM[:, q, 0:T], ps, AF.Tanh, [bank], [bTM[q]], scale=0.5)
                act(TM2[:, q, 0:T], ps, AF.Identity, [bank], [bTM2[q]], scale=0.5)
            else:
                act(TM[:, q, 0:T], ps, AF.Tanh, [bank, B["CONST"]], [bTM[q]], scale=0.5, bias=hb)
                act(TM2[:, q, 0:T], ps, AF.Identity, [bank, B["CONST"]], [bTM2[q]], scale=0.5, bias=hb)
            stt(dst, TM[:, q, 0:T], 1.0, TM2[:, q, 0:T], ALU.add, ALU.mult, [bTM[q], bTM2[q]], [dstbuf])

        def in_proj(T, s, t0, tile_idx, is_last, kout, vout):
            nch = T // 64
            rot = [0]

            def nb():
                b = rot[0] % 4
                rot[0] += 1
                return b

            def fm_block(kind):
                wv, wb, _ = w_next()
                for cc in range(4):
                    b = nb()
                    for k in range(8):
                        mm(PS[b][:, 0:T], wv[:, cc * 1024 + k * 128:cc * 1024 + (k + 1) * 128], H[:, k, 0:T],
                           k == 0, k == 7, [wb, bH[k]], bPS[b])
                    src = PS[b][:, 0:T]
                    if kind == "k":
                        h = cc
                        kb = bKT[h][tile_idx]
                        cp("act", KT[:, h, t0:t0 + T], src, [bPS[b]], [kb])
                        q = h % 2
                        cp("dve", KST[:, q, 0:T], src, [bPS[b]], [bKST[q]])
                        dma("sp", kout[h * 128:(h + 1) * 128, t0:t0 + T], KST[:, q, 0:T], "ko", [bKST[q]], [])
                    elif kind == "q":
                        cp("act", QT[:, cc, 0:T], src, [bPS[b]], [bQT[cc]])
                    elif kind in ("x0", "x1"):
                        c = (0 if kind == "x0" else 4) + cc
                        cp("act", XBC[:, c, 3:3 + T], src, [bPS[b]], [bXBC[c]])
                        if is_last:
                            cp("dve", CVO[:, c * 3:c * 3 + 3], PS[b][:, T - 3:T], [bPS[b]], [B["CVO"]])
                    else:
                        silu_from_psum(T, src, bPS[b], ZS[:, cc, 0:T], bZS[cc], cc % 2)

            fm_block("x0")
            fm_block("x1")
            for c in range(8):
                cp("dve", XBC[:, c, 0:3], CVP[:, c * 3:c * 3 + 3], [B["CVP"]], [bXBC[c]])
            conv(T)
            if not is_last:
                for c in range(8):
                    cp("dve", CVP[:, c * 3:c * 3 + 3], XBC[:, c, T:T + 3], [bXBC[c]], [B["CVP"]])
            wv, wb, _ = w_next()
            b = nb()
            for ch in range(nch):
                for k in range(8):
                    mm(PS[b][0:64, ch * 8:ch * 8 + 8], H[:, k, ch * 64:(ch + 1) * 64],
                       wv[:, 4096 + k * 8:4096 + (k + 1) * 8], k == 0, k == 7, [wb, bH[k]], bPS[b])
            tt("dve", DTR[:, 0:nch, :], PS[b][0:64, 0:nch * 8].rearrange("p (c h) -> p c h", h=8),
               VEC[0:64, V_DTB:V_DTB + 8].unsqueeze(1).to_broadcast([64, nch, 8]), ALU.add,
               [bPS[b], B["VEC"]], [B["DTR"]])
            act(DTR[:, 0:nch, :], DTR[:, 0:nch, :], AF.Exp, [B["DTR"]], [B["DTR"]])
            act(DT[:, 0:nch, :], DTR[:, 0:nch, :], AF.Ln, [B["DTR"]], [B["DT"]], bias=1.0)
            tt("dve", DA[:, 0:nch, :], DT[:, 0:nch, :], AN[0:64, :].unsqueeze(1).to_broadcast([64, nch, 8]), ALU.mult,
               [B["DT"], B["AN"]], [B["DA"]])
            cp("dve", DAH[:, 0:nch, :], DA[:, 0:nch, :], [B["DA"]], [B["DAH"]])
            tt("dve", DAL[:, 0:nch, :], DA[:, 0:nch, :], DAH[:, 0:nch, :], ALU.subtract, [B["DA"], B["DAH"]], [B["DAL"]])
            ntb = (T + 127) // 128
            for tb in range(ntb):
                nt_ = min(128, T - tb * 128)
                b = nb()
                for k in range(8):
                    mm(PS[b][0:nt_, :], H[:, k, tb * 128:tb * 128 + nt_], wv[:, k * 512:(k + 1) * 512],
                       k == 0, k == 7, [wb, bH[k]], bPS[b])
                blkid = (t0 // 128) + tb
                cp("act", VC[0:nt_, blkid, :], PS[b][0:nt_, :], [bPS[b]], [bVC[blkid]])
                q = tb % 2
                cp("dve", VST[0:nt_, q, :], PS[b][0:nt_, :], [bPS[b]], [bVST[q]])
                dma("sp", vout[t0 + tb * 128:t0 + tb * 128 + nt_, :], VST[0:nt_, q, :], "vo", [bVST[q]], [])
            fm_block("k")
            fm_block("q")
            fm_block("z")

        def conv(T):
            for c in range(8):
                b = 4 + c % 2
                for w in range(4):
                    mm(PS[b][:, 0:T], DG[:, c * 4 + w, :], XBC[:, c, w:w + T], w == 0, w == 3,
                       [B["DG"], bXBC[c]], bPS[b])
                if c < 4:
                    dst, db = XSb[:, c, 0:T], bXS[c]
                elif c < 6:
                    dst, db = BTb[:, c - 4, 0:T], bBT[c - 4]
                else:
                    dst, db = CTb[:, c - 6, 0:T], bCT[c - 6]
                silu_from_psum(T, PS[b][:, 0:T], bPS[b], dst, db, c % 2, hb=HCB[:, c:c + 1])

        GTb = CONB[:, C_GT:C_GT + 64]
        I64b = CONB[:, C_ID:C_ID + 64]
        NEGb = CONB[:, C_NEG:C_NEG + 512]

        def pb(name, q):
            return B[name if q == 0 else name + "1"]

        def ssd_1(ch):
            q = ch % 2
            cs = slice(ch * 64, (ch + 1) * 64)
            dA = DA[:, ch, :]
            tt("dve", RTH[:].rearrange("p (h l) -> p h l", h=8), TRI.unsqueeze(1).to_broadcast([64, 8, 64]),
               DAH[:, ch, :].unsqueeze(2).to_broadcast([64, 8, 64]), ALU.mult, [B["CON"], B["DAH"]], [B["RTH"]])
            tt("dve", RTL[:].rearrange("p (h l) -> p h l", h=8), TRI.unsqueeze(1).to_broadcast([64, 8, 64]),
               DAL[:, ch, :].unsqueeze(2).to_broadcast([64, 8, 64]), ALU.mult, [B["CON"], B["DAL"]], [B["RTL"]])
            P.group_begin("pe")
            mm(PS[0][0:64, :], GTb, RTH[:], True, False, [B["CONB"], B["RTH"]], bPS[0])
            mm(PS[0][0:64, :], GTb, RTL[:], False, False, [B["CONB"], B["RTL"]], bPS[0])
            mm(PS[0][0:64, :], I64b, NEGb, False, True, [B["CONB"]], bPS[0])
            mm(PS[1][:, :], ONEB[0:64, :], RTH[:], True, False, [B["CONST"], B["RTH"]], bPS[1])
            mm(PS[1][:, :], ONEB[0:64, :], RTL[:], False, True, [B["CONST"], B["RTL"]], bPS[1])
            mm(PS[2][0:64, 0:8], GT, dA, True, True, [B["CON"], B["DA"]], bPS[2])
            mm(PS[2][:, 8:16], ONEF[:], dA, True, True, [B["CONST"], B["DA"]], bPS[2])
            for g in range(2):
                mm(PS[2][0:64, 64 + g * 64:128 + g * 64], BTb[:, g, cs], CTb[:, g, cs], True, True,
                   [bBT[g], bCT[g]], bPS[2])
            P.group_end()
            act(LM[:, q, :], PS[0][0:64, :], AF.Exp, [bPS[0]], [pb("LM", q)])
            act(EA[:, q, :], PS[1][:, :], AF.Exp, [bPS[1]], [pb("EA", q)])
            act(DD[:, q, :], PS[2][0:64, 0:8], AF.Exp, [bPS[2]], [pb("DD", q)])
            act(CD[:, ch % 3, :], PS[2][:, 8:16], AF.Exp, [bPS[2]], [B["CD" + "abc"[ch % 3]]])
            cp("act", CBs[:, q, :], PS[2][0:64, 64:192], [bPS[2]], [pb("CBs", q)])

        def ssd_2a(ch):
            q = ch % 2
            cs = slice(ch * 64, (ch + 1) * 64)
            ps3 = PS[3][:].bitcast(BF16)
            P.group_begin("pe")
            for c in range(4):
                tr(ps3[0:64, c * 128:(c + 1) * 128], XSb[:, c, cs], IDB[:], [bXS[c], B["CONST"]], bPS[3])
            for g in range(2):
                tr(ps3[0:64, 512 + g * 128:512 + (g + 1) * 128], BTb[:, g, cs], IDB[:], [bBT[g], B["CONST"]], bPS[3])
            P.group_end()
            tt("dve", DD[:, q, :], DD[:, q, :], DT[:, ch, :], ALU.mult, [pb("DD", q), B["DT"]], [pb("DD", q)])
            tt("dve", XD[:, q, :].rearrange("p (h l) -> p h l", h=8), ps3[0:64, 0:512].rearrange("p (h l) -> p h l", h=8),
               DD[:, q, :].unsqueeze(2).to_broadcast([64, 8, 64]), ALU.mult, [bPS[3], pb("DD", q)], [pb("XD", q)])
            cp("act", XTk[:, q, :], ps3[0:64, 0:512], [bPS[3]], [pb("XTk", q)])
            cp("act", BTK[:, q, :], ps3[0:64, 512:768], [bPS[3]], [pb("BTK", q)])

        def ssd_2b(ch):
            q = ch % 2
            cs = slice(ch * 64, (ch + 1) * 64)
            tt("dve", LM[:, q, :].rearrange("p (g r l) -> p g r l", g=2, r=4),
               LM[:, q, :].rearrange("p (g r l) -> p g r l", g=2, r=4),
               CBs[:, q, :].rearrange("p (g l) -> p g l", g=2).unsqueeze(2).to_broadcast([64, 2, 4, 64]), ALU.mult,
               [pb("LM", q), pb("CBs", q)], [pb("LM", q)])
            tt("dve", WT[:, q, :].rearrange("p (h l) -> p h l", h=8), LM[:, q, :].rearrange("p (h l) -> p h l", h=8),
               DT[:, ch, :].unsqueeze(2).to_broadcast([64, 8, 64]), ALU.mult, [pb("LM", q), B["DT"]], [pb("WT", q)])
            tt("dve", CP[:, q, :].rearrange("p (g r l) -> p g r l", g=2, r=4),
               EA[:, q, :].rearrange("p (g r l) -> p g r l", g=2, r=4),
               CTb[:, :, cs].unsqueeze(2).to_broadcast([128, 2, 4, 64]), ALU.mult, [pb("EA", q), bCT[0], bCT[1]], [pb("CP", q)])

        def ssd_3(ch):
            q = ch % 2
            cs = slice(ch * 64, (ch + 1) * 64)
            P.group_begin("pe")
            for g in range(2):
                mm(PS[5][:, g * 256:(g + 1) * 256], BTK[:, q, g * 128:(g + 1) * 128], XD[:, q, g * 256:(g + 1) * 256],
                   True, True, [pb("BTK", q), pb("XD", q)], bPS[5])
            P.group_end()
            P.group_begin("pe")
            for h in range(8):
                po = (h % 2) * 64
                out = PS[4][po:po + 64, (h // 2) * 64:(h // 2) * 64 + 64]
                mm(out, XTk[:, q, h * 64:(h + 1) * 64], WT[:, q, h * 64:(h + 1) * 64], True, False,
                   [pb("XTk", q), pb("WT", q)], bPS[4])
                mm(out, SBb[:, h * 64:(h + 1) * 64], CP[:, q, h * 64:(h + 1) * 64], False, True,
                   [B["SBb"], pb("CP", q)], bPS[4])
            P.group_end()
            tt("dve", S[:].rearrange("p (h l) -> p h l", h=8), S[:].rearrange("p (h l) -> p h l", h=8),
               CD[:, ch % 3, :].unsqueeze(2).to_broadcast([128, 8, 64]), ALU.mult, [B["S"], B["CD" + "abc"[ch % 3]]], [B["S"]])
            tt("dve", S[:], S[:], PS[5][:, :], ALU.add, [B["S"], bPS[5]], [B["S"]])
            cp("act", SBb[:], S[:], [B["S"]], [B["SBb"]])
            for c in range(4):
                cp("act", YS[:, c, cs], PS[4][:, c * 64:(c + 1) * 64], [bPS[4]], [bYS[c]])

        def ssd_post(T):
            for c in range(4):
                stt(YS[:, c, 0:T], XSb[:, c, 0:T], VEC[:, V_DS + c:V_DS + c + 1], YS[:, c, 0:T], ALU.mult, ALU.add,
                    [bXS[c], B["VEC"], bYS[c]], [bYS[c]])
                tt("dve", YS[:, c, 0:T], YS[:, c, 0:T], ZS[:, c, 0:T], ALU.mult, [bYS[c], bZS[c]], [bYS[c]])
                k = c % 2
                act(EB[:, k, 0:T], YS[:, c, 0:T], AF.Square, [bYS[c]], [bE[k]])
                mm(PS[7][:, 0:T], ONEB[:], EB[:, k, 0:T], c == 0, c == 3, [B["CONST"], bE[k]], bPS[7])
            act(RS[:, 0:T], PS[7][:, 0:T], AF.Ln, [bPS[7], B["CONST"]], [B["RS"]], bias=MHALF[:, 0:1], scale=1.0 / 512)
            act(RS[:, 0:T], RS[:, 0:T], AF.Exp, [B["RS"]], [B["RS"]], scale=-0.5)
            for c in range(4):
                stt(H[:, c, 0:T], YS[:, c, 0:T], VEC[:, V_SN + c:V_SN + c + 1], RS[:, 0:T], ALU.mult, ALU.mult,
                    [bYS[c], B["VEC"], B["RS"]], [bH[c]])

        def attention(T, kblocks):
            nb_ = len(kblocks)
            pending = [None]
            for h in range(4):
                def emit_S(bi, h=h):
                    kc0, nk, vblk, pbase, q0, ktile = kblocks[bi]
                    P.group_begin("pe")
                    for j in range(2):
                        sb_ = (bi % 2) * 2 + j
                        mm(PS[sb_][pbase:pbase + nk, q0:T], KT[j * 64:(j + 1) * 64, h, kc0:kc0 + nk],
                           QT[j * 64:(j + 1) * 64, h, q0:T], True, True, [bKT[h][ktile], bQT[h]], bPS[sb_])
                    P.group_end()
                    s0 = (bi % 2) * 2
                    act(EB[pbase:pbase + nk, s0:s0 + 2, q0:T], PSALL[pbase:pbase + nk, s0:s0 + 2, q0:T], AF.Exp,
                        [bPS[s0], bPS[s0 + 1]], [bE[s0], bE[s0 + 1]], scale=0.125)

                def emit_AV(bi, h=h):
                    kc0, nk, vblk, pbase, q0, ktile = kblocks[bi]
                    first, last = bi == 0, bi == nb_ - 1
                    P.group_begin("pe")
                    for j in range(2):
                        e_i = (bi % 2) * 2 + j
                        mm(PS[4 + j][:, q0:T], VC[pbase:pbase + nk, vblk, h * 128:(h + 1) * 128],
                           EB[pbase:pbase + nk, e_i, q0:T], first, last, [bVC[vblk], bE[e_i]], bPS[4 + j])
                    for j in range(2):
                        e_i = (bi % 2) * 2 + j
                        mm(PS[6 + j][:, q0:T], ONEB[pbase:pbase + nk, :], EB[pbase:pbase + nk, e_i, q0:T],
                           first, last, [B["CONST"], bE[e_i]], bPS[6 + j])
                    P.group_end()

                for i in range(nb_ + 1):
                    if i < nb_:
                        emit_S(i)
                    if i >= 1:
                        emit_AV(i - 1)
                    if i == 1 and pending[0] is not None:
                        pending[0]()
                        pending[0] = None
                for j in range(2):
                    act(OAv[j][:, 0:T], PS[6 + j][:, 0:T], AF.Ln, [bPS[6 + j]], [bOA[j]])
                for j in range(2):
                    act(OAv[j][:, 0:T], OAv[j][:, 0:T], AF.Exp, [bOA[j]], [bOA[j]], scale=-1.0)
                    tt("dve", OAv[j][:, 0:T], PS[4 + j][:, 0:T], OAv[j][:, 0:T], ALU.mult, [bPS[4 + j], bOA[j]], [bOA[j]])
                stt(OAv[2][:, 0:T], OAv[1][:, 0:T], NLAM[:, 0:1], OAv[0][:, 0:T], ALU.mult, ALU.add,
                    [bOA[0], bOA[1], B["NLAM"]], [bOA[2]])

                def part2(h=h):
                    act(SQB[:, 0:T], OAv[2][:, 0:T], AF.Square, [bOA[2]], [B["SQB"]])
                    mm(PS[0][:, 0:T], ONEB[:], SQB[:, 0:T], True, True, [B["CONST"], B["SQB"]], bPS[0])
                    act(RS[:, 0:T], PS[0][:, 0:T], AF.Ln, [bPS[0], B["CONST"]], [B["RS"]], bias=MHALF[:, 0:1], scale=1.0 / 128)
                    act(RS[:, 0:T], RS[:, 0:T], AF.Exp, [B["RS"]], [B["RS"]], scale=-0.5)
                    stt(H[:, 4 + h, 0:T], OAv[2][:, 0:T], 1.0 - LAMBDA_INIT, RS[:, 0:T], ALU.mult, ALU.mult,
                        [bOA[2], B["RS"]], [bH[4 + h]])
                pending[0] = part2
            pending[0]()

        def out_proj(T, s):
            for blk in range(2):
                wv, wb, _ = w_next()
                for cc in range(4):
                    m = blk * 4 + cc
                    b = m % 2
                    for k in range(8):
                        mm(PS[b][:, 0:T], wv[:, cc * 1024 + k * 128:cc * 1024 + (k + 1) * 128], H[:, k, 0:T],
                           k == 0, k == 7, [wb, bH[k]], bPS[b])
                    stt(X[:, m, 0:T], PS[b][:, 0:T], PV[:, s, 4, m:m + 1], X[:, m, 0:T], ALU.mult, ALU.add,
                        [bPS[b], B["PV"], bX[m]], [bX[m]])

        def final_norm(T, yout, t0, next_x=None):
            for c in range(8):
                k = c % 2
                act(EB[:, k, 0:T], X[:, c, 0:T], AF.Square, [bX[c]], [bE[k]])
                mm(PS[7][:, 0:T], ONEB[:], EB[:, k, 0:T], c == 0, c == 7, [B["CONST"], bE[k]], bPS[7])
            act(RS[:, 0:T], PS[7][:, 0:T], AF.Ln, [bPS[7], B["CONST"]], [B["RS"]], bias=MHALF[:, 0:1], scale=1.0 / D)
            act(RS[:, 0:T], RS[:, 0:T], AF.Exp, [B["RS"]], [B["RS"]], scale=-0.5)
            P.guard(bYO, bA + mixer_bufs)
            for c in range(8):
                stt(YO[:, c, 0:T], X[:, c, 0:T], VEC[:, V_FN + c:V_FN + c + 1], RS[:, 0:T], ALU.mult, ALU.mult,
                    [bX[c], B["VEC"], B["RS"]], [bYO[c]])
                if next_x is not None and c % 2 == 1:
                    xn, Tn, t0n = next_x
                    i4 = c // 2
                    dma("sp", X[:, 2 * i4:2 * i4 + 2, 0:Tn],
                        xn.rearrange("(c p) t -> p c t", p=128)[:, 2 * i4:2 * i4 + 2, t0n:t0n + Tn], "x%d" % i4, [],
                        bX[2 * i4:2 * i4 + 2])
            dma("sp", yout.rearrange("(c p) t -> p c t", p=128)[:, :, t0:t0 + T], YO[:, :, 0:T], "yo", bYO, [])
            P.guard(bA + mixer_bufs, bYO)

        def tile(T, s, xin, t0, tile_idx, is_last, kblocks, yout, kout, vout, next_x=None, x_loaded=False):
            xin_v = xin.rearrange("(c p) t -> p c t", p=128)
            if not x_loaded:
                for i4 in range(4):
                    dma("sp", X[:, 2 * i4:2 * i4 + 2, 0:T], xin_v[:, 2 * i4:2 * i4 + 2, t0:t0 + T], "x%d" % i4, [],
                        bX[2 * i4:2 * i4 + 2])
            rms_to_H(T, s, 0)
            P.guard(bA, mixer_bufs)
            ffn(T, s, 0)
            rms_to_H(T, s, 1)
            P.guard(mixer_bufs, bA)
            in_proj(T, s, t0, tile_idx, is_last, kout, vout)
            nch_ = T // 64
            for i in range(nch_ + 2):
                if 1 <= i <= nch_:
                    ssd_2a(i - 1)
                if i >= 2:
                    ssd_3(i - 2)
                if i < nch_:
                    ssd_1(i)
                if 1 <= i <= nch_:
                    ssd_2b(i - 1)
            ssd_post(T)
            attention(T, kblocks)
            out_proj(T, s)
            rms_to_H(T, s, 2)
            P.guard(bA, mixer_bufs)
            ffn(T, s, 2)
            final_norm(T, yout, t0, next_x)

        mset("dve", S[:], 0.0, [B["S"]])
        mset("dve", SBb[:], 0.0, [B["SBb"]])
        mset("dve", CVP[:], 0.0, [B["CVP"]])
        for t in range(NT):
            kbl = [(128 * kb, 128, kb, 0, 0, kb // 4) for kb in range(4 * t)]
            kbl += [(TT * t + 64 * c, 64, 4 * t + c // 2, (c % 2) * 64, 64 * c, t) for c in range(8)]
            if t + 1 < NT:
                nx = (xT, TT, (t + 1) * TT)
            else:
                nx = (xsT, SS, 0) if with_sample else None
            tile(TT, 0, xT, t * TT, t, t == NT - 1, kbl, yT, kTo, vo, next_x=nx, x_loaded=(t > 0))
        dma("sp", ssm_o, S[:], "misc", [B["S"]], [])
        dma("sp", conv_o, CVO[:], "misc", [B["CVO"]], [])

        if with_sample:
            for h in range(4):
                dma("pool", KT[:, h, 2048:4096], ckT[h * 128:(h + 1) * 128, :], "cache", [], bKT[h][4:8])
            for i in range(4):
                dma("pool", VC[:, 16 + 4 * i:20 + 4 * i, :],
                    cv[i * 512:(i + 1) * 512, :].rearrange("(b p) e -> p b e", p=128), "cache", [],
                    bVC[16 + 4 * i:20 + 4 * i])
            dma("sp", S[:], sssm, "misc", [], [B["S"]])
            cp("act", SBb[:], S[:], [B["S"]], [B["SBb"]])
            dma("sp", CVI[:], sconv, "misc", [], [B["CVI"]])
            cp("dve", CVP[:], CVI[:], [B["CVI"]], [B["CVP"]])
            kbl = [(2048 + 128 * i, 128, 16 + i, 0, 0, 4 + i // 4) for i in range(16)] + [(0, 64, 0, 0, 0, 0)]
            tile(SS, 1, xsT, 0, 0, True, kbl, ysT, ksTo, vso, x_loaded=True)
            dma("sp", ssms_o, S[:], "misc", [B["S"]], [])
            dma("sp", convs_o, CVO[:], "misc", [B["CVO"]], [])

        P.emit(nc, block, sems, lane_sems)
    return nc


def _chunk128(v):
    return np.ascontiguousarray(v.reshape(-1, 128).T)


def _lhs_chunks(w):
    K, M = w.shape
    out = []
    for m in range(M // 128):
        blk = w[:, m * 128:(m + 1) * 128].reshape(K // 128, 128, 128).transpose(1, 0, 2).reshape(128, -1)
        out.append(blk)
    return out


def _build_wall(ffn1_wgu, ffn1_wd, w_in, w_out, ffn2_wgu, ffn2_wd):
    def ffn_blocks(wgu, wd):
        g = _lhs_chunks(wgu[:, :DFF])
        u = _lhs_chunks(wgu[:, DFF:])
        blks = []
        for jb in range(11):
            blks.append(np.concatenate([g[2 * jb], u[2 * jb], g[2 * jb + 1], u[2 * jb + 1]], axis=1))
        blks += _lhs_chunks(wd)
        return blks
    z, xbc, dt, q, k, v = (w_in[:, 0:512], w_in[:, 512:1536], w_in[:, 1536:1544], w_in[:, 1544:2056],
                           w_in[:, 2056:2568], w_in[:, 2568:3080])
    fm = _lhs_chunks(xbc) + _lhs_chunks(k) + _lhs_chunks(q) + _lhs_chunks(z)
    fmb = [np.concatenate(fm[4 * i:4 * i + 4], axis=1) for i in range(5)]
    vb = v.reshape(8, 128, 512).transpose(1, 0, 2).reshape(128, -1)
    dtb = dt.reshape(8, 128, 8).transpose(1, 0, 2).reshape(128, -1)
    inb = [fmb[0], fmb[1], np.concatenate([vb, dtb], axis=1), fmb[2], fmb[3], fmb[4]]
    om = _lhs_chunks(w_out)
    outb = [np.concatenate(om[0:4], axis=1), np.concatenate(om[4:8], axis=1)]
    allb = ffn_blocks(ffn1_wgu, ffn1_wd) + inb + outb + ffn_blocks(ffn2_wgu, ffn2_wd)
    assert [b.shape[1] for b in allb] == BLKS
    return np.ascontiguousarray(np.concatenate(allb, axis=1), dtype=np.float32)


def _consts():
    con = np.zeros((128, NCON), np.float32)
    con[:, C_ID:C_ID + 128] = np.eye(128, dtype=np.float32)
    t = np.arange(64)
    con[0:64, C_GT:C_GT + 64] = (t[:, None] > t[None, :]).astype(np.float32)
    con[0:64, C_TRI:C_TRI + 64] = (t[:, None] <= t[None, :]).astype(np.float32)
    neg = np.where(t[:, None] > t[None, :], -30000.0, 0.0).astype(np.float32)
    con[0:64, C_NEG:C_NEG + 64] = neg
    return con


_NC_CACHE = {}


def _prep(inputs, NT):
    f = lambda a: np.asarray(a, dtype=np.float32)
    wall = _build_wall(f(inputs["ffn1_wgu"])[0], f(inputs["ffn1_wd"])[0], f(inputs["w_in"])[0],
                       f(inputs["w_out"])[0], f(inputs["ffn2_wgu"])[0], f(inputs["ffn2_wd"])[0])
    con = _consts()
    vec = np.zeros((128, NVEC), np.float32)
    vec[:, V_N1:V_N1 + 8] = _chunk128(f(inputs["norm1"])[0])
    vec[:, V_N2:V_N2 + 8] = _chunk128(f(inputs["norm2"])[0])
    vec[:, V_N3:V_N3 + 8] = _chunk128(f(inputs["norm3"])[0])
    vec[:, V_FN:V_FN + 8] = _chunk128(f(inputs["final_norm"]))
    vec[:, V_CB:V_CB + 8] = _chunk128(f(inputs["conv_b"])[0])
    cw = f(inputs["conv_w"])[0]
    for w in range(4):
        vec[:, V_CW + w * 8:V_CW + w * 8 + 8] = _chunk128(cw[w])
    vec[:, V_SN:V_SN + 4] = _chunk128(f(inputs["ssd_norm"])[0])
    vec[:, V_DS:V_DS + 4] = _chunk128(np.repeat(f(inputs["d_skip"])[0], 64))
    vec[:, V_DTB:V_DTB + 8] = f(inputs["dt_bias"])[0][None, :]
    vec[:, V_ALOG:V_ALOG + 8] = f(inputs["a_log"])[0][None, :]
    lam = np.concatenate([f(inputs[n])[0] for n in ("lam_q1", "lam_k1", "lam_q2", "lam_k2")])
    vec[:, V_LAM:V_LAM + 256] = lam[None, :]
    vec[:, V_BADA:V_BADA + 72] = _chunk128(f(inputs["b_ada"])[0])
    wada = np.ascontiguousarray(f(inputs["w_ada"])[0])
    xp, xs = f(inputs["x_prompt"]), f(inputs["x_sample"])
    cp_, cs_ = f(inputs["c_prompt"]), f(inputs["c_sample"])
    ck, cvv = f(inputs["cache_k"])[0], f(inputs["cache_v"])[0]
    st, sc = f(inputs["state_ssm"])[0], f(inputs["state_conv"])[0]
    maps = []
    for b in range(8):
        cT = np.stack([_chunk128(cp_[b]), _chunk128(cs_[b])], axis=2).reshape(128, 16)
        maps.append({
            "xT": np.ascontiguousarray(xp[b, :NT * TT].T),
            "xsT": np.ascontiguousarray(xs[b].T),
            "cT": np.ascontiguousarray(cT),
            "wada": wada, "vec": vec, "con": con, "wall": wall,
            "ckT": np.ascontiguousarray(ck[b].reshape(PAST, 512).T),
            "cv": np.ascontiguousarray(cvv[b].reshape(PAST, 512)),
            "sssm": np.ascontiguousarray(st[b].reshape(512, 128).T),
            "sconv": np.ascontiguousarray(sc[b].reshape(3, 8, 128).transpose(2, 1, 0).reshape(128, 24)),
        })
    return maps


def _gather(res, NT):
    L = NT * TT
    g = lambda n: [np.asarray(r[n], dtype=np.float32) for r in res]
    y = np.stack([a.T for a in g("yT")])
    ys = np.stack([a.T for a in g("ysT")])
    nk = np.stack([a.T.reshape(L, 4, 2, 64) for a in g("kTo")])[None]
    nv = np.stack([a.reshape(L, 4, 128) for a in g("vo")])[None]
    ssm = np.stack([a.T.reshape(8, 64, 128) for a in g("ssm_o")])[None]
    conv = np.stack([a.reshape(128, 8, 3).transpose(2, 1, 0).reshape(3, 1024) for a in g("conv_o")])[None]
    nks = np.stack([a.T.reshape(SS, 4, 2, 64) for a in g("ksTo")])[None]
    nvs = np.stack([a.reshape(SS, 4, 128) for a in g("vso")])[None]
    ssms = np.stack([a.T.reshape(8, 64, 128) for a in g("ssms_o")])[None]
    convs = np.stack([a.reshape(128, 8, 3).transpose(2, 1, 0).reshape(3, 1024) for a in g("convs_o")])[None]
    return tuple(np.ascontiguousarray(a, dtype=np.float32) for a in
                 (y, ys, nk, nv, ssm, conv, nks, nvs, ssms, convs))


def run(inputs, NT=8, cores=8):
    maps = _prep(inputs, NT)[:cores]
    nc = build(NT)
    res = run_bass_kernel_spmd(nc, maps, core_ids=list(range(cores)))
    return _gather(res.results, NT)


def kernel(**inputs):
    return run(inputs, 8, 8)
```
